# Optimizing a Trainium2 kernel written in Bass

```python
import jax, jax.numpy as jnp
from jax import lax
import numpy as np

D_MODEL = 1024
BATCH = 4
SEQ = 8192
DEPTH = 4

N_MIXERS = 3
HEAD_DIM = 64
EPS = 1e-6
NEG_INF = -1e30

A_HEADS = D_MODEL // HEAD_DIM
A_WINDOWS = (128, 512, 2048)
A_DILATIONS = (1, 4, 16)
A_GROUPS = len(A_WINDOWS)
ROPE_THETA = 500000.0
ROPE_DIMS = HEAD_DIM // 4

B_CONV_WIDTH = 3

C_Q_HEADS = D_MODEL // HEAD_DIM
C_KV_HEADS = 4
C_THETA = 10000.0
Q_BLOCK = 128
GRID_W = 64

D_FF = 4 * D_MODEL

N_A = len(range(0, DEPTH, N_MIXERS))
N_B = len(range(1, DEPTH, N_MIXERS))
N_C = len(range(2, DEPTH, N_MIXERS))

kernel_name = "hybrid_dilated_conv_axial_gqa_encoder"


def rms_norm(x, g):
    xf = x.astype(jnp.float32)
    y = xf * lax.rsqrt(jnp.mean(xf * xf, axis=-1, keepdims=True) + EPS)
    return (y * g.astype(jnp.float32)).astype(x.dtype)


def rope_angles(pos, dim, theta):
    inv = theta ** (-jnp.arange(0, dim, 2, dtype=jnp.float32) / dim)
    ang = pos.astype(jnp.float32)[:, None] * inv[None, :]
    return jnp.cos(ang), jnp.sin(ang)


def apply_rope(x, cos, sin):
    half = x.shape[-1] // 2
    xf = x.astype(jnp.float32)
    x1, x2 = xf[..., :half], xf[..., half:]
    c, s = cos[:, None, :], sin[:, None, :]
    return jnp.concatenate([x1 * c - x2 * s, x2 * c + x1 * s], axis=-1).astype(x.dtype)


def partial_rope(x, cos, sin):
    return jnp.concatenate([apply_rope(x[..., :ROPE_DIMS], cos, sin), x[..., ROPE_DIMS:]], axis=-1)


def dilated_group(q, k, v, dilation, radius):
    B, S, H, Dh = q.shape
    L = S // dilation
    nb = -(-L // radius)
    Lp = nb * radius

    def split(t):
        t = jnp.moveaxis(t.reshape(B, L, dilation, H, Dh), 2, 1)
        return jnp.pad(t, ((0, 0), (0, 0), (0, Lp - L), (0, 0), (0, 0)))

    def band(t):
        tp = jnp.pad(t, ((0, 0), (0, 0), (radius, radius), (0, 0), (0, 0)))
        tb = tp.reshape(B, dilation, nb + 2, radius, H, Dh)
        return jnp.concatenate([tb[:, :, :-2], tb[:, :, 1:-1], tb[:, :, 2:]], axis=3)

    qb = split(q).reshape(B, dilation, nb, radius, H, Dh)
    kb = band(split(k))
    vb = band(split(v))

    q_idx = jnp.arange(Lp).reshape(nb, radius)
    k_idx = jnp.arange(nb)[:, None] * radius - radius + jnp.arange(3 * radius)[None, :]
    mask = ((jnp.abs(q_idx[:, :, None] - k_idx[:, None, :]) <= radius)
            & (k_idx[:, None, :] >= 0) & (k_idx[:, None, :] < L))

    s = jnp.einsum('bdnqhe,bdnkhe->bdnqhk', qb, kb).astype(jnp.float32) * (HEAD_DIM ** -0.5)
    s = jnp.where(mask[None, None, :, :, None, :], s, NEG_INF)
    mx = jnp.max(s, axis=-1)
    p = jnp.exp(s - mx[..., None])
    den = jnp.sum(p, axis=-1)
    num = jnp.einsum('bdnqhk,bdnkhe->bdnqhe', p, vb.astype(jnp.float32))

    def unsplit(t):
        rest = t.shape[4:]
        t = t.reshape((B, dilation, Lp) + rest)[:, :, :L]
        return jnp.moveaxis(t, 1, 2).reshape((B, S) + rest)

    return unsplit(num), unsplit(den), unsplit(mx)


def mixer_a(h, w_qkv, q_gain, k_gain, w_o, cos, sin):
    B, S, _ = h.shape
    qkv = (h @ w_qkv).reshape(B, S, A_GROUPS, 3, A_HEADS, HEAD_DIM)
    nums, dens, mxs = [], [], []
    for g in range(A_GROUPS):
        dil = A_DILATIONS[g]
        radius = A_WINDOWS[g] // (2 * dil)
        q = partial_rope(rms_norm(qkv[:, :, g, 0], q_gain[g]), cos, sin)
        k = partial_rope(rms_norm(qkv[:, :, g, 1], k_gain[g]), cos, sin)
        v = qkv[:, :, g, 2]
        num, den, mx = dilated_group(q, k, v, dil, radius)
        nums.append(num); dens.append(den); mxs.append(mx)
    mx = jnp.stack(mxs, 0)
    wts = jnp.exp(mx - jnp.max(mx, axis=0, keepdims=True))
    num = jnp.sum(wts[..., None] * jnp.stack(nums, 0), axis=0)
    den = jnp.sum(wts * jnp.stack(dens, 0), axis=0)
    o = (num / den[..., None]).astype(h.dtype).reshape(B, S, D_MODEL)
    return o @ w_o


def mixer_b(h, w_in, conv_w, w_out):
    b_gate, c_gate, xt = jnp.split(h @ w_in, 3, axis=-1)
    u = c_gate * xt
    y = lax.conv_general_dilated(u, conv_w[:, None, :].astype(u.dtype), window_strides=(1,),
                                 padding=((1, 1),), dimension_numbers=('NWC', 'WIO', 'NWC'),
                                 feature_group_count=D_MODEL)
    return (b_gate * y) @ w_out


def axial_rope(x, cos_r, sin_r, cos_c, sin_c):
    half = HEAD_DIM // 2
    return jnp.concatenate([apply_rope(x[..., :half], cos_r, sin_r),
                            apply_rope(x[..., half:], cos_c, sin_c)], axis=-1)


def mixer_c(h, w_qkv, q_gain, k_gain, w_o):
    B, S, _ = h.shape
    rows = S // GRID_W
    row_id = jnp.repeat(jnp.arange(rows), GRID_W)
    col_id = jnp.tile(jnp.arange(GRID_W), rows)
    cos_r, sin_r = rope_angles(row_id, HEAD_DIM // 2, C_THETA)
    cos_c, sin_c = rope_angles(col_id, HEAD_DIM // 2, C_THETA)

    qkv = h @ w_qkv
    nq, nk = C_Q_HEADS * HEAD_DIM, C_KV_HEADS * HEAD_DIM
    q = qkv[..., :nq].reshape(B, S, C_Q_HEADS, HEAD_DIM)
    k = qkv[..., nq:nq + nk].reshape(B, S, C_KV_HEADS, HEAD_DIM)
    v = qkv[..., nq + nk:].reshape(B, S, C_KV_HEADS, HEAD_DIM)
    q = axial_rope(rms_norm(q, q_gain), cos_r, sin_r, cos_c, sin_c)
    k = axial_rope(rms_norm(k, k_gain), cos_r, sin_r, cos_c, sin_c)

    grp = C_Q_HEADS // C_KV_HEADS
    nblk = S // Q_BLOCK
    qb = jnp.moveaxis(q.reshape(B, nblk, Q_BLOCK, C_KV_HEADS, grp, HEAD_DIM), 1, 0)

    def attend(qblk):
        s = jnp.einsum('bqhgd,bkhd->bhgqk', qblk, k).astype(jnp.float32) * (HEAD_DIM ** -0.5)
        p = jax.nn.softmax(s, axis=-1)
        return jnp.einsum('bhgqk,bkhd->bqhgd', p.astype(v.dtype), v)

    o = lax.map(attend, qb)
    o = jnp.moveaxis(o, 0, 1).reshape(B, S, D_MODEL)
    return o @ w_o


def mlp_sq_relu(h, w1, w2):
    return jnp.square(jax.nn.relu(h @ w1)) @ w2


def setup_inputs(seed: int = 0) -> dict:
    key = jax.random.key(seed)
    ks = jax.random.split(key, 20)
    D, Dh = D_MODEL, HEAD_DIM
    nrm = lambda k, shape, fan_in: jax.random.normal(k, shape, jnp.float32) * (fan_in ** -0.5)
    gain = lambda k, shape: 1.0 + 0.02 * jax.random.normal(k, shape, jnp.float32)
    c_cols = (C_Q_HEADS + 2 * C_KV_HEADS) * Dh
    return {
        "x": jax.random.normal(ks[0], (BATCH, SEQ, D), jnp.float32),
        "norm1": gain(ks[1], (DEPTH, D)),
        "norm2": gain(ks[2], (DEPTH, D)),
        "a_wqkv": nrm(ks[3], (N_A, D, A_GROUPS * 3 * D), D),
        "a_q_gain": gain(ks[4], (N_A, A_GROUPS, Dh)),
        "a_k_gain": gain(ks[5], (N_A, A_GROUPS, Dh)),
        "a_wo": nrm(ks[6], (N_A, D, D), D),
        "b_win": nrm(ks[7], (N_B, D, 3 * D), D),
        "b_conv": nrm(ks[8], (N_B, B_CONV_WIDTH, D), B_CONV_WIDTH),
        "b_wout": nrm(ks[9], (N_B, D, D), D),
        "c_wqkv": nrm(ks[10], (N_C, D, c_cols), D),
        "c_q_gain": gain(ks[11], (N_C, Dh)),
        "c_k_gain": gain(ks[12], (N_C, Dh)),
        "c_wo": nrm(ks[13], (N_C, D, D), D),
        "mlp_w1": nrm(ks[14], (DEPTH, D, D_FF), D),
        "mlp_w2": nrm(ks[15], (DEPTH, D_FF, D), D_FF),
    }


def reference(x, norm1, norm2, a_wqkv, a_q_gain, a_k_gain, a_wo, b_win, b_conv, b_wout,
              c_wqkv, c_q_gain, c_k_gain, c_wo, mlp_w1, mlp_w2):
    S = x.shape[1]
    cos, sin = rope_angles(jnp.arange(S), ROPE_DIMS, ROPE_THETA)
    h = x
    for i in range(DEPTH):
        kind, j = i % N_MIXERS, i // N_MIXERS
        y = rms_norm(h, norm1[i])
        if kind == 0:
            y = mixer_a(y, a_wqkv[j], a_q_gain[j], a_k_gain[j], a_wo[j], cos, sin)
        elif kind == 1:
            y = mixer_b(y, b_win[j], b_conv[j], b_wout[j])
        else:
            y = mixer_c(y, c_wqkv[j], c_q_gain[j], c_k_gain[j], c_wo[j])
        h = h + y
        h = h + mlp_sq_relu(rms_norm(h, norm2[i]), mlp_w1[i], mlp_w2[i])
    return h
```

```python
import contextlib
import numpy as np
import ml_dtypes
import concourse.bass as bass
import concourse.mybir as mybir
from concourse.bass_utils import run_bass_kernel_spmd

F32 = mybir.dt.float32
BF16 = mybir.dt.bfloat16
AF = mybir.ActivationFunctionType
ALU = mybir.AluOpType
NPBF = ml_dtypes.bfloat16

DM = 1024
SEQ = 8192
NB = 4
NCORES = 8
TOK = 4096
HALO = 1024
EXT = TOK + 2 * HALO
EPS = 1e-6
DFF = 4096
A_DIL = (1, 4, 16)


class Buf:
    __slots__ = ("w", "r", "name")

    def __init__(self, name=""):
        self.w = {}
        self.r = {}
        self.name = name


class Sched:
    CE = ("pe", "act", "dve", "pool")

    def __init__(self, nc, es, ndq=14):
        self.nc = nc
        self.q = {e: [] for e in ("pe", "act", "dve", "pool", "sp")}
        self.sem = {}
        self.cnt = {}
        for e in self.CE:
            self.sem[e] = es.enter_context(nc.semaphore("s_" + e))
            self.cnt[e] = 0
        self.dq = {}
        for qn in ("sp", "pool", "act"):
            ks = []
            for i in range(ndq):
                k = "d%s%d" % (qn, i)
                self.sem[k] = es.enter_context(nc.semaphore(k))
                self.cnt[k] = 0
                ks.append(k)
            self.dq[qn] = [ks, 0]
        self.seen = {e: {} for e in self.q}
        self.nwaits = 0

    def _need(self, eng, reads, writes):
        need = {}
        for b in reads:
            for k, v in b.w.items():
                if need.get(k, 0) < v:
                    need[k] = v
        for b in writes:
            for k, v in b.w.items():
                if need.get(k, 0) < v:
                    need[k] = v
            for k, v in b.r.items():
                if need.get(k, 0) < v:
                    need[k] = v
        waits = []
        seen = self.seen[eng]
        for k, v in need.items():
            if k == "pe" and eng == "pe":
                continue
            if seen.get(k, 0) >= v:
                continue
            seen[k] = v
            waits.append((k, v))
        self.nwaits += len(waits)
        return waits

    def op(self, eng, emit, reads=(), writes=(), signal=True):
        waits = self._need(eng, reads, writes)
        if signal:
            self.cnt[eng] += 1
            v = self.cnt[eng]
        else:
            v = self.cnt[eng] + 1
        for b in reads:
            if b.r.get(eng, 0) < v:
                b.r[eng] = v
        for b in writes:
            b.w = {eng: v}
            b.r = {}
        self.q[eng].append((waits, emit, eng if signal else None, 1))

    def dma(self, qn, out, in_, reads=(), writes=(), slow=False):
        waits = self._need(qn, reads, writes)
        ks, i = self.dq[qn]
        k = ks[i % len(ks)]
        self.dq[qn][1] = i + 1
        if self.cnt[k] > self.seen[qn].get(k, 0):
            self.seen[qn][k] = self.cnt[k]
            waits.append((k, self.cnt[k]))
        self.cnt[k] += 16
        v = self.cnt[k]
        for b in reads:
            if b.r.get(k, 0) < v:
                b.r[k] = v
        for b in writes:
            b.w = {k: v}
            b.r = {}
        if slow:
            self.q[qn].append((waits, (lambda e, o=out, i_=in_: e.dma_start(out=o, in_=i_, allow_slow_non_contiguous=True)), k, 16))
        else:
            self.q[qn].append((waits, (lambda e, o=out, i_=in_: e.dma_start(out=o, in_=i_)), k, 16))

    def barrier(self):
        for e in self.q:
            waits = []
            for k, v in self.cnt.items():
                if k == "pe" and e == "pe":
                    continue
                if v > self.seen[e].get(k, 0):
                    self.seen[e][k] = v
                    waits.append((k, v))
            if waits:
                self.q[e].append((waits, None, None, 0))

    def emit(self, block):
        def run(name):
            def f(e):
                for waits, emit, k, inc in self.q[name]:
                    for wk, wv in waits:
                        e.wait_ge(self.sem[wk], wv)
                    if emit is not None:
                        ins = emit(e)
                        if k is not None:
                            ins.then_inc(self.sem[k], inc)
            return f

        block.tensor(run("pe"))
        block.scalar(run("act"))
        block.vector(run("dve"))
        block.gpsimd(run("pool"))
        block.sync(run("sp"))


ARENA_WORDS = 50000


class KB:
    def __init__(self):
        self.nc = bass.Bass("TRN2", target_bir_lowering=False)
        self.es = contextlib.ExitStack()
        nc = self.nc
        self.S = Sched(nc, self.es)
        self.arena = self.es.enter_context(nc.sbuf_tensor("arena", [128, ARENA_WORDS], F32))
        self.ps2 = [self.es.enter_context(nc.psum_tensor("ps%d" % i, [128, 1024], F32)) for i in range(4)]
        self.ps = [self.ps2[i // 2][:, (i % 2) * 512:(i % 2 + 1) * 512] for i in range(8)]
        self.psb = [Buf("ps%d" % i) for i in range(8)]
        self.off = 0
        self.base = 0
        self.dram_n = 0
        self.consts = {}

    def f32(self, n):
        assert self.off + n <= ARENA_WORDS, ("sbuf overflow", self.off, n)
        ap = self.arena[:, self.off:self.off + n]
        self.off += n
        return ap

    def bf(self, n):
        w = (n + 1) // 2
        return self.f32(w).bitcast(BF16)[:, 0:n]

    def phase_end(self):
        self.S.barrier()
        self.off = self.base

    def dram(self, shape, dt, name=None, kind="Internal"):
        self.dram_n += 1
        nm = name or ("scr%d" % self.dram_n)
        return self.nc.dram_tensor(nm, list(shape), dt, kind=kind).ap()

    def mm(self, out, lhsT, rhs, start, stop, reads, writes, signal=True):
        self.S.op("pe", lambda e: e.matmul(out, lhsT, rhs, start=start, stop=stop, skip_group_check=True),
                  reads, writes, signal)

    def tr(self, out, in_, ident, reads, writes, signal=True):
        self.S.op("pe", lambda e: e.transpose(out, in_, ident), reads, writes, signal)

    def act(self, out, in_, func, reads, writes, bias=None, scale=None, accum_out=None):
        kw = {}
        if bias is not None:
            kw["bias"] = bias
        if scale is not None:
            kw["scale"] = scale
        if accum_out is not None:
            kw["accum_out"] = accum_out
        self.S.op("act", lambda e: e.activation(out, in_, func, **kw), reads, writes)

    def tt(self, eng, out, in0, in1, op, reads, writes):
        self.S.op(eng, lambda e: e.tensor_tensor(out, in0, in1, op), reads, writes)

    def ts(self, eng, out, in0, s1, s2, op0, op1, reads, writes):
        if s2 is None:
            self.S.op(eng, lambda e: e.tensor_scalar(out, in0, s1, None, op0), reads, writes)
        else:
            self.S.op(eng, lambda e: e.tensor_scalar(out, in0, s1, s2, op0, op1), reads, writes)

    def stt(self, eng, out, in0, scalar, in1, op0, op1, reads, writes):
        self.S.op(eng, lambda e: e.scalar_tensor_tensor(out, in0, scalar, in1, op0, op1), reads, writes)

    def copy(self, eng, out, in_, reads, writes):
        if eng == "act":
            self.S.op("act", lambda e: e.copy(out, in_), reads, writes)
        else:
            self.S.op(eng, lambda e: e.tensor_copy(out, in_), reads, writes)

    def memset(self, eng, out, val, writes):
        self.S.op(eng, lambda e: e.memset(out, val), (), writes)

    def dma(self, q, out, in_, reads=(), writes=(), slow=False):
        self.S.dma(q, out, in_, reads, writes, slow)

    def cvt(self, src, rows_per=None):
        K, N = src.shape
        dst = self.dram([K, N], BF16)
        if rows_per is None:
            rows_per = 16
            while rows_per * 2 <= min(K, (1 << 20) // N) and K % (rows_per * 2) == 0:
                rows_per *= 2
        bufs = []
        for r0 in range(0, K, rows_per):
            b = Buf("cvt")
            self.dma("pool", dst[r0:r0 + rows_per, :], src[r0:r0 + rows_per, :], (), (b,))
            bufs.append(b)
        return dst, bufs

    def load_consts(self, cd):
        for name, (ap, dt, cols) in cd.items():
            t = self.bf(cols) if dt == BF16 else self.f32(cols)
            b = Buf(name)
            self.dma("sp", t, ap, (), (b,))
            self.consts[name] = (t, b)
        t = self.f32(2)
        b = Buf("eps")
        self.memset("dve", t[:, 0:1], EPS, (b,))
        self.memset("dve", t[:, 1:2], 64 * EPS, (b,))
        self.consts["eps"] = (t, b)
        self.base = self.off

    def norm_T(self, ht, hb, nsub, g32, g32b, yT, yTb, scr):
        ident, identb = self.consts["ident"]
        ss, ssb, rr, rrb, ytok, ytokb, pst = scr
        for j in range(nsub):
            self.act(ytok[:, j, :], ht[:, j, :], AF.Square, (hb,), (ytokb[j], ssb[j]), accum_out=ss[:, j:j + 1])
            self.act(rr[:, j:j + 1], ss[:, j:j + 1], AF.Sqrt, (ssb[j],), (rrb[j],), bias=self.consts["eps"][0][:, 0:1],
                     scale=1.0 / DM)
            self.S.op("dve", lambda e, o=rr[:, j:j + 1]: e.reciprocal(o, o), (rrb[j],), (rrb[j],))
            self.stt("dve", ytok[:, j, :], ht[:, j, :], rr[:, j:j + 1], g32, ALU.mult, ALU.mult,
                     (hb, rrb[j], g32b), (ytokb[j],))
            bank = pst[j % len(pst)]
            psv = self.ps[bank][:].bitcast(BF16).rearrange("p (c t) -> p c t", t=128)
            for c in range(8):
                self.tr(psv[:, c, :], ytok[:, j, c * 128:(c + 1) * 128], ident,
                        (ytokb[j], identb), (self.psb[bank],), signal=(c == 7))
            self.copy("act" if j % 2 == 0 else "dve", yT[:, :, j * 128:(j + 1) * 128], psv,
                      (self.psb[bank],), (yTb[j],))

    def norm_scratch(self, pst):
        ss = self.f32(4)
        rr = self.f32(4)
        ytok = self.bf(4 * 1024).rearrange("p (j d) -> p j d", d=1024)
        return (ss, [Buf("ss") for _ in range(4)], rr, [Buf("rr") for _ in range(4)],
                ytok, [Buf("ytok") for _ in range(4)], pst)

    def load_g32(self, g_ap):
        g = self.f32(1024)
        gb = Buf("g")
        self.dma("sp", g, g_ap.partition_broadcast(128), (), (gb,))
        return g, gb

    def phase_out_ffn(self, h_in, row0, aT, wo, wo_bufs, g2, w1, w1_bufs, w2, w2_bufs, h_out, dbg=None, ntile=8, out_row0=0):
        S = self.S
        g32, g32b = self.load_g32(g2)
        wos = None
        if aT is not None:
            wos = self.bf(8 * 1024).rearrange("p (k n) -> p k n", n=1024)
            wosb = Buf("wo")
            self.dma("sp", wos, wo.rearrange("(k p) n -> p k n", p=128), tuple(wo_bufs), (wosb,))
        NSLOT = 2
        wsl = [self.bf(16384) for _ in range(NSLOT)]
        wslb = [[Buf("ws") for _ in range(4)] for _ in range(NSLOT)]
        hts = [self.f32(4096).rearrange("p (j d) -> p j d", d=1024) for _ in range(2)]
        htb = [Buf("ht") for _ in range(2)]
        yT = self.bf(4096).rearrange("p (c t) -> p c t", t=512)
        yTb = [Buf("yT") for _ in range(4)]
        hid = self.bf(32 * 512).rearrange("p (m t) -> p m t", t=512)
        hidb = [Buf("hid") for _ in range(32)]
        relu = [self.f32(512) for _ in range(2)]
        relub = [Buf("relu") for _ in range(2)]
        if aT is not None:
            ats = [self.bf(4096).rearrange("p (c t) -> p c t", t=512) for _ in range(2)]
            atb = [Buf("at") for _ in range(2)]
        scr = self.norm_scratch([0, 1])
        slot_i = [0]

        def load_w(kind, idx):
            s = slot_i[0] % NSLOT
            slot_i[0] += 1
            if kind == "w1":
                v = wsl[s].rearrange("p (q k n) -> p q k n", q=4, k=8)
                for qd in range(4):
                    self.dma("sp" if qd % 2 == 0 else "pool", v[:, qd, :, :],
                             w1[:, idx * 2048 + qd * 512: idx * 2048 + (qd + 1) * 512].rearrange("(k p) n -> p k n", p=128),
                             tuple(w1_bufs), (wslb[s][qd],))
            else:
                v = wsl[s].rearrange("p (k n) -> p k n", n=512)
                for qd in range(4):
                    self.dma("sp" if qd % 2 == 0 else "pool", v[:, qd * 8:(qd + 1) * 8, :],
                             w2[qd * 1024:(qd + 1) * 1024, idx * 512:(idx + 1) * 512].rearrange("(k p) n -> p k n", p=128),
                             tuple(w2_bufs), (wslb[s][qd],))
            return v, wslb[s]

        def load_h(i):
            r = row0 + i * 512
            self.dma("sp", hts[i % 2], h_in[r:r + 512, :].rearrange("(j p) d -> p j d", p=128), (), (htb[i % 2],))
            if aT is not None:
                self.dma("sp", ats[i % 2], aT[:, i * 512:(i + 1) * 512].rearrange("(c p) t -> p c t", p=128),
                         (), (atb[i % 2],))

        load_h(0)
        pending_w = [load_w("w1", 0)]
        psrot = [0]

        def nbank():
            b = 2 + psrot[0] % 6
            psrot[0] += 1
            return b

        for i in range(ntile):
            ht, hb = hts[i % 2], htb[i % 2]
            if i + 1 < ntile:
                load_h(i + 1)
            if aT is not None:
                at, ab = ats[i % 2], atb[i % 2]
                for j in range(4):
                    for f in range(2):
                        bk = nbank()
                        for kc in range(8):
                            self.mm(self.ps[bk][:, :], at[:, kc, j * 128:(j + 1) * 128], wos[:, kc, f * 512:(f + 1) * 512],
                                    kc == 0, kc == 7, (ab, wosb), (self.psb[bk],), signal=(kc == 7))
                        self.tt("dve", ht[:, j, f * 512:(f + 1) * 512], self.ps[bk][:, :], ht[:, j, f * 512:(f + 1) * 512],
                                ALU.add, (self.psb[bk], hb), (hb,))
            self.norm_T(ht, hb, 4, g32, g32b, yT, yTb, scr)
            if dbg is not None and i == 0:
                self.dma("sp", dbg["yT"], yT.rearrange("p c t -> p (c t)"), tuple(yTb), ())
                self.dma("sp", dbg["rr"], scr[2], tuple(scr[3]), ())
                self.dma("sp", dbg["ss"], scr[0], tuple(scr[1]), ())
            for wg in range(2):
                wv, wb = pending_w.pop(0)
                pending_w.append(load_w("w1", 1) if wg == 0 else load_w("w2", 0))
                for ml in range(16):
                    m = wg * 16 + ml
                    bk = nbank()
                    for kc in range(8):
                        self.mm(self.ps[bk][:, :], wv[:, ml // 4, kc, (ml % 4) * 128:(ml % 4 + 1) * 128], yT[:, kc, :],
                                kc == 0, kc == 7, (wb[ml // 4],) + tuple(yTb), (self.psb[bk],), signal=(kc == 7))
                    rl, rlb = relu[m % 2], relub[m % 2]
                    self.act(rl, self.ps[bk][:, :], AF.Relu, (self.psb[bk],), (rlb,))
                    self.tt("pool", hid[:, m, :], rl, rl, ALU.mult, (rlb,), (hidb[m],))
            if dbg is not None and i == 0:
                self.dma("sp", dbg["hid"], hid.rearrange("p m t -> p (m t)"), tuple(hidb), ())
            for f in range(2):
                wv, wb = pending_w.pop(0)
                if f == 0:
                    pending_w.append(load_w("w2", 1))
                elif i + 1 < ntile:
                    pending_w.append(load_w("w1", 0))
                for j in range(4):
                    bk = nbank()
                    for m in range(32):
                        self.mm(self.ps[bk][:, :], hid[:, m, j * 128:(j + 1) * 128], wv[:, m, :],
                                m == 0, m == 31, (hidb[m], wb[m // 8]), (self.psb[bk],), signal=(m == 31))
                    self.tt("dve", ht[:, j, f * 512:(f + 1) * 512], self.ps[bk][:, :], ht[:, j, f * 512:(f + 1) * 512],
                            ALU.add, (self.psb[bk], hb), (hb,))
            self.dma("sp", h_out[out_row0 + i * 512:out_row0 + (i + 1) * 512, :].rearrange("(j p) d -> p j d", p=128), ht, (hb,), ())
        self.phase_end()


    def h_tiles(self, h_ext, tiles, g1, pst=(0, 1)):
        g, gb = self.load_g32(g1)
        hts = [self.f32(4096).rearrange("p (j d) -> p j d", d=1024) for _ in range(2)]
        htb = [Buf("ht") for _ in range(2)]
        yT = self.bf(4096).rearrange("p (c t) -> p c t", t=512)
        yTb = [Buf("yT") for _ in range(4)]
        scr = self.norm_scratch(list(pst))

        def load(i):
            r0, nsub = tiles[i]
            self.dma("sp", hts[i % 2][:, 0:nsub, :], h_ext[r0:r0 + 128 * nsub, :].rearrange("(j p) d -> p j d", p=128),
                     (), (htb[i % 2],))

        load(0)
        for i in range(len(tiles)):
            if i + 1 < len(tiles):
                load(i + 1)
            self.norm_T(hts[i % 2], htb[i % 2], tiles[i][1], g, gb, yT, yTb, scr)
            yield i, yT, yTb

    def load_w_resident(self, w, w_bufs, ncols):
        wt = self.bf(8 * ncols).rearrange("p (k n) -> p k n", n=ncols)
        wb = []
        step = 512
        for c0 in range(0, ncols, step):
            b = Buf("wres")
            self.dma("sp" if (c0 // step) % 2 == 0 else "pool", wt[:, :, c0:c0 + step],
                     w[:, c0:c0 + step].rearrange("(k p) n -> p k n", p=128), tuple(w_bufs), (b,))
            wb.append(b)
        return wt, wb

    def load_gain(self, t, col, vec64, b):
        v = vec64.rearrange("(d o) -> d o", o=1)
        self.dma("sp", t[0:64, col:col + 1], v, (), (b,))
        self.dma("sp", t[64:128, col:col + 1], v, (), (b,))

    def qk_s1(self, psq, N, scr):
        sq, sqb = scr[0], scr[1]
        self.act(sq[:, 0:N], self.ps[psq][:, 0:N], AF.Square, (self.psb[psq],), (sqb,))

    def qk_s2(self, psq, N, gain, gainb, scr):
        sq, sqb, rs, rsb, qn, qnb, t1, t1b, t2, t2b, bss, brot = scr
        onesbd, onesb = self.consts["onesbd"]
        eps = self.consts["eps"][0]
        self.mm(self.ps[bss][:, 0:N], onesbd, sq[:, 0:N], True, True, (sqb, onesb), (self.psb[bss],))
        self.act(rs[:, 0:N], self.ps[bss][:, 0:N], AF.Sqrt, (self.psb[bss],), (rsb,), bias=eps[:, 0:1], scale=1.0 / 64)
        self.S.op("dve", lambda e, o=rs[:, 0:N]: e.reciprocal(o, o), (rsb,), (rsb,))
        self.stt("dve", qn[:, 0:N], self.ps[psq][:, 0:N], gain, rs[:, 0:N], ALU.mult, ALU.mult,
                 (self.psb[psq], gainb, rsb), (qnb,))

    def qk_s3(self, N, rotT, rotb, ctab, stab, tabb, scr, out_ap, outb, in_view=None):
        sq, sqb, rs, rsb, qn, qnb, t1, t1b, t2, t2b, bss, brot = scr
        self.mm(self.ps[brot][:, 0:N], rotT, qn[:, 0:N], True, True, (qnb, rotb), (self.psb[brot],))
        self.tt("pool", t1[:, 0:N], qn[:, 0:N], ctab, ALU.mult, (qnb, tabb), (t1b,))
        self.tt("dve", t2[:, 0:N], self.ps[brot][:, 0:N], stab, ALU.mult, (self.psb[brot], tabb), (t2b,))
        if in_view is None:
            self.tt("pool", out_ap, t1[:, 0:N], t2[:, 0:N], ALU.add, (t1b, t2b), (outb,))
        else:
            self.tt("pool", out_ap, in_view(t1[:, 0:N]), in_view(t2[:, 0:N]), ALU.add, (t1b, t2b), (outb,))

    def qk_pipe_push(self, pipe, job):
        pipe.append(job)
        n = len(pipe) - 1
        if job is not None:
            job["s1"]()
        if n - 1 >= 0 and pipe[n - 1] is not None:
            pipe[n - 1]["s2"]()
        if n - 2 >= 0 and pipe[n - 2] is not None:
            pipe[n - 2]["s3"]()
            if pipe[n - 2].get("done"):
                pipe[n - 2]["done"]()

    def qk_pipe_flush(self, pipe):
        self.qk_pipe_push(pipe, None)
        self.qk_pipe_push(pipe, None)

    def qk_scratch(self, bss, brot):
        sq = self.bf(512)
        rs = self.f32(512)
        qn = self.bf(512)
        t1 = self.f32(512)
        t2 = self.f32(512)
        return (sq, Buf("sq"), rs, Buf("rs"), qn, Buf("qn"), t1, Buf("t1"), t2, Buf("t2"), bss, brot)

    def phase_m1_b(self, h_ext, g1, win, win_bufs, uT, bT, ntile=8):
        tiles = [(HALO + i * 512, 4) for i in range(ntile)] + [(HALO - 128, 1), (HALO + ntile * 512, 1)]
        ucol = [128 + i * 512 for i in range(ntile)] + [0, 128 + ntile * 512]
        wt, wb = self.load_w_resident(win, win_bufs, 3072)
        tmp = [self.f32(512) for _ in range(2)]
        tmpb = [Buf("tmp") for _ in range(2)]
        ut = [self.bf(4096).rearrange("p (m t) -> p m t", t=512) for _ in range(2)]
        utb = [Buf("ut") for _ in range(2)]
        bt = [self.bf(4096).rearrange("p (m t) -> p m t", t=512) for _ in range(2)]
        btb = [Buf("bt") for _ in range(2)]
        rot = [0]

        def nbank():
            b = 2 + rot[0] % 6
            rot[0] += 1
            return b

        def proj(col0, N, yT, yTb):
            bk = nbank()
            for kc in range(8):
                self.mm(self.ps[bk][:, 0:N], wt[:, kc, col0:col0 + 128], yT[:, kc, 0:N], kc == 0, kc == 7,
                        (wb[col0 // 512],) + tuple(yTb), (self.psb[bk],), signal=(kc == 7))
            return bk

        for i, yT, yTb in self.h_tiles(h_ext, tiles, g1):
            nsub = tiles[i][1]
            N = 128 * nsub
            u, ub = ut[i % 2], utb[i % 2]
            b_, bb = bt[i % 2], btb[i % 2]
            for m in range(8):
                bc = proj(1024 + m * 128, N, yT, yTb)
                bx = proj(2048 + m * 128, N, yT, yTb)
                tp, tpb = tmp[m % 2], tmpb[m % 2]
                self.copy("act", tp[:, 0:N], self.ps[bc][:, 0:N], (self.psb[bc],), (tpb,))
                self.tt("dve", u[:, m, 0:N], self.ps[bx][:, 0:N], tp[:, 0:N], ALU.mult, (self.psb[bx], tpb), (ub,))
                if nsub == 4:
                    bg = proj(m * 128, N, yT, yTb)
                    self.copy("act", b_[:, m, :], self.ps[bg][:, :], (self.psb[bg],), (bb,))
            self.dma("sp", uT[:, ucol[i]:ucol[i] + N].rearrange("(m p) t -> p m t", p=128), u[:, :, 0:N], (ub,), ())
            if nsub == 4:
                self.dma("sp", bT[:, i * 512:(i + 1) * 512].rearrange("(m p) t -> p m t", p=128), b_, (bb,), ())
        self.phase_end()

    def phase_conv_b(self, uT, bT, conv, zT, ntile=8):
        cw = self.f32(24).rearrange("p (m k) -> p m k", k=3)
        cwb = Buf("cw")
        for k in range(3):
            self.dma("sp", cw[:, :, k:k + 1], conv[k, :].rearrange("(m p o) -> p m o", p=128, o=1), (), (cwb,), slow=True)
        uh = [self.bf(8 * 516).rearrange("p (m t) -> p m t", t=516) for _ in range(2)]
        uhb = [Buf("uh") for _ in range(2)]
        bt = [self.bf(4096).rearrange("p (m t) -> p m t", t=512) for _ in range(2)]
        btb = [Buf("bt") for _ in range(2)]
        zt = [self.bf(4096).rearrange("p (m t) -> p m t", t=512) for _ in range(2)]
        ztb = [Buf("zt") for _ in range(2)]
        acc = [self.f32(512) for _ in range(2)]
        accb = [Buf("acc") for _ in range(2)]

        def load(i):
            self.dma("sp", uh[i % 2][:, :, 0:514], uT[:, 127 + i * 512: 127 + i * 512 + 514].rearrange("(m p) t -> p m t", p=128),
                     (), (uhb[i % 2],))
            self.dma("sp", bt[i % 2], bT[:, i * 512:(i + 1) * 512].rearrange("(m p) t -> p m t", p=128), (), (btb[i % 2],))

        load(0)
        for i in range(ntile):
            if i < ntile - 1:
                load(i + 1)
            u, ub = uh[i % 2], uhb[i % 2]
            for m in range(8):
                a, ab = acc[m % 2], accb[m % 2]
                self.ts("dve", a, u[:, m, 0:512], cw[:, m, 0:1], None, ALU.mult, None, (ub, cwb), (ab,))
                self.stt("dve", a, u[:, m, 1:513], cw[:, m, 1:2], a, ALU.mult, ALU.add, (ub, cwb, ab), (ab,))
                self.stt("dve", a, u[:, m, 2:514], cw[:, m, 2:3], a, ALU.mult, ALU.add, (ub, cwb, ab), (ab,))
                self.tt("pool", zt[i % 2][:, m, :], a, bt[i % 2][:, m, :], ALU.mult, (ab, btb[i % 2]), (ztb[i % 2],))
            self.dma("sp", zT[:, i * 512:(i + 1) * 512].rearrange("(m p) t -> p m t", p=128), zt[i % 2], (ztb[i % 2],), ())
        self.phase_end()


    def phase_m1_c(self, h_ext, g1, wqkv, w_bufs, qg, kg, cosT, sinT, QT, KT, V, n_own=8):
        tiles = [(i * 512, 4) for i in range(16)]
        wt, wb = self.load_w_resident(wqkv, w_bufs, 1536)
        gains = self.f32(2)
        gb = Buf("gains")
        self.load_gain(gains, 0, qg, gb)
        self.load_gain(gains, 1, kg, gb)
        rotT, rotb = self.consts["rotC"]
        tabs = [self.f32(1024) for _ in range(3)]
        tabb = [Buf("tab") for _ in range(3)]
        qt = [self.bf(4096).rearrange("p (c t) -> p c t", t=512) for _ in range(2)]
        qtb = [Buf("qt") for _ in range(2)]
        kt = [self.bf(1024).rearrange("p (c t) -> p c t", t=512) for _ in range(2)]
        ktb = [Buf("kt") for _ in range(2)]
        vt = [self.bf(1024).rearrange("p (j n) -> p j n", n=256) for _ in range(2)]
        vtb = [Buf("vt") for _ in range(2)]
        scrs = [self.qk_scratch(4, 5), self.qk_scratch(4, 5), self.qk_scratch(4, 5)]
        rot = [0]
        ci = [0]
        pipe = []

        def load_tab(i):
            self.dma("sp", tabs[i % 3][:, 0:512], cosT[:, i * 512:(i + 1) * 512], (), (tabb[i % 3],))
            self.dma("sp", tabs[i % 3][:, 512:1024], sinT[:, i * 512:(i + 1) * 512], (), (tabb[i % 3],))

        load_tab(0)
        for i, yT, yTb in self.h_tiles(h_ext, tiles, g1, pst=(0,)):
            if i + 1 < 16:
                load_tab(i + 1)
            tab, tb = tabs[i % 3], tabb[i % 3]
            own = i < n_own
            chunks = ([("q", c) for c in range(8)] if own else []) + [("k", 0), ("k", 1)]
            for kind, c in chunks:
                col0 = c * 128 if kind == "q" else 1024 + c * 128
                bk = 1 + rot[0] % 3
                rot[0] += 1
                scr = scrs[ci[0] % 3]
                ci[0] += 1
                for kc in range(8):
                    self.mm(self.ps[bk][:, :], wt[:, kc, col0:col0 + 128], yT[:, kc, :], kc == 0, kc == 7,
                            (wb[col0 // 512],) + tuple(yTb), (self.psb[bk],), signal=(kc == 7))
                gcol = 0 if kind == "q" else 1
                o_ap, o_b = (qt[i % 2][:, c, :], qtb[i % 2]) if kind == "q" else (kt[i % 2][:, c, :], ktb[i % 2])
                job = {"s1": (lambda bk=bk, scr=scr: self.qk_s1(bk, 512, scr)),
                       "s2": (lambda bk=bk, scr=scr, gcol=gcol: self.qk_s2(bk, 512, gains[:, gcol:gcol + 1], gb, scr)),
                       "s3": (lambda scr=scr, tab=tab, tb=tb, o_ap=o_ap, o_b=o_b: self.qk_s3(
                           512, rotT, rotb, tab[:, 0:512], tab[:, 512:1024], tb, scr, o_ap, o_b))}
                if (kind, c) == chunks[-1]:
                    def done(i=i, own=own):
                        if own:
                            self.dma("sp", QT[:, i * 512:(i + 1) * 512].rearrange("(c p) t -> p c t", p=128), qt[i % 2], (qtb[i % 2],), ())
                        self.dma("sp", KT[:, i * 512:(i + 1) * 512].rearrange("(c p) t -> p c t", p=128), kt[i % 2], (ktb[i % 2],), ())
                    job["done"] = done
                self.qk_pipe_push(pipe, job)
            for j in range(4):
                bk = 6 + j % 2
                for kc in range(8):
                    self.mm(self.ps[bk][:, 0:256], yT[:, kc, j * 128:(j + 1) * 128], wt[:, kc, 1280:1536], kc == 0, kc == 7,
                            (wb[2],) + tuple(yTb), (self.psb[bk],), signal=(kc == 7))
                self.copy("act", vt[i % 2][:, j, :], self.ps[bk][:, 0:256], (self.psb[bk],), (vtb[i % 2],))
            self.dma("sp", V[i * 512:(i + 1) * 512, :].rearrange("(j p) n -> p j n", p=128), vt[i % 2], (vtb[i % 2],), ())
        self.qk_pipe_flush(pipe)
        self.phase_end()

    def phase_attn_c(self, QT, KT, V, aT, n_q=8):
        NKB = 64
        K2 = [self.bf(8192) for _ in range(2)]
        k2b = [Buf("k2") for _ in range(2)]
        VA = [self.bf(8192).rearrange("p (k n) -> p k n", n=128) for _ in range(2)]
        VB = [self.bf(8192).rearrange("p (k n) -> p k n", n=128) for _ in range(2)]
        vab = [Buf("va") for _ in range(2)]
        vbb = [Buf("vb") for _ in range(2)]
        for s_ in range(2):
            self.memset("pool", VA[s_][:, :, 64:128], 1.0, (vab[s_],))
            self.memset("pool", VB[s_][:, :, 0:64], 1.0, (vbb[s_],))
        Qs = [self.bf(512) for _ in range(2)]
        qsb = [Buf("qs") for _ in range(2)]
        NP = 8
        Pp = [self.bf(1024) for _ in range(NP // 2)]
        Ps = [Pp[i // 2][:, (i % 2) * 512:(i % 2 + 1) * 512] for i in range(NP)]
        psb_ = [Buf("P") for _ in range(NP)]
        num = self.f32(512)
        numb = Buf("num")
        denx = self.f32(512)
        denxb = Buf("denx")
        den = self.f32(512)
        denb = Buf("den")
        ao = [self.bf(512) for _ in range(2)]
        aob = [Buf("ao") for _ in range(2)]

        def load_kv(j):
            s_ = j % 2
            self.dma("sp", K2[s_][0:64, :], KT[j * 64:(j + 1) * 64, :], (), (k2b[s_],))
            self.dma("pool", K2[s_][64:128, :], KT[j * 64:(j + 1) * 64, :], (), (k2b[s_],))
            vsrc = V[:, j * 64:(j + 1) * 64].rearrange("(k p) d -> p k d", p=128)
            self.dma("sp", VA[s_][:, :, 0:64], vsrc, (), (vab[s_],))
            self.dma("pool", VB[s_][:, :, 64:128], vsrc, (), (vbb[s_],))

        units = [(j, hp, qc) for j in range(4) for hp in (2 * j, 2 * j + 1) for qc in range(n_q)]
        steps = [(u, hh, kb) for u in range(len(units)) for kb in range(NKB) for hh in range(2)]
        LA = 4

        def load_q(u):
            j, hp, qc = units[u]
            self.dma("sp", Qs[u % 2], QT[hp * 128:(hp + 1) * 128, qc * 512:(qc + 1) * 512], (), (qsb[u % 2],))

        load_kv(0)
        load_q(0)
        def emit_qk(idx):
            u, hh, kb = steps[idx]
            j, hp, qc = units[u]
            s_ = j % 2
            if hh == 0 and kb == 0:
                if u + 1 < len(units):
                    load_q(u + 1)
            sb = idx % 4
            pr = slice(hh * 64, (hh + 1) * 64)
            self.mm(self.ps[sb][:, :], K2[s_][pr, kb * 128:(kb + 1) * 128], Qs[u % 2][pr, :], True, True,
                    (k2b[s_], qsb[u % 2]), (self.psb[sb],))

        def emit_exp(idx):
            sb = idx % 4
            self.act(Ps[idx % NP], self.ps[sb][:, :], AF.Exp, (self.psb[sb],), (psb_[idx % NP],), scale=0.125)

        def emit_pv(i2):
            u, hh, kb = steps[i2]
            j, hp, qc = units[u]
            s_ = j % 2
            if hh == 0 and kb == 0 and hp == 2 * j and qc == 0 and j + 1 < 4:
                load_kv(j + 1)
            po = 4 + (u % 2) * 2 + hh
            vv, vvb = (VA[s_], vab[s_]) if hh == 0 else (VB[s_], vbb[s_])
            self.mm(self.ps[po][:, :], vv[:, kb, :], Ps[i2 % NP], kb == 0, kb == NKB - 1,
                    (vvb, psb_[i2 % NP]), (self.psb[po],), signal=(kb == NKB - 1))
            if hh == 1 and kb == NKB - 1:
                pa, pb = 4 + (u % 2) * 2, 4 + (u % 2) * 2 + 1
                self.copy("dve", num[0:64, :], self.ps[pa][0:64, :], (self.psb[pa],), (numb,))
                self.copy("dve", denx[64:128, :], self.ps[pa][64:128, :], (self.psb[pa],), (denxb,))
                self.copy("dve", num[64:128, :], self.ps[pb][64:128, :], (self.psb[pb],), (numb,))
                self.copy("dve", denx[0:64, :], self.ps[pb][0:64, :], (self.psb[pb],), (denxb,))
                self.dma("sp", den[0:64, :], denx[64:128, :], (denxb,), (denb,))
                self.dma("sp", den[64:128, :], denx[0:64, :], (denxb,), (denb,))
                self.S.op("dve", lambda e, o=den: e.reciprocal(o, o), (denb,), (denb,))
                self.tt("dve", ao[u % 2], num, den, ALU.mult, (numb, denb), (aob[u % 2],))
                self.dma("sp", aT[hp * 128:(hp + 1) * 128, qc * 512:(qc + 1) * 512], ao[u % 2], (aob[u % 2],), ())

        for base in range(0, len(steps) + LA, 2):
            for idx in (base, base + 1):
                if idx < len(steps):
                    emit_qk(idx)
            if base + 1 < len(steps):
                sb = base % 4
                self.act(Pp[(base % NP) // 2], self.ps2[sb // 2][:, :], AF.Exp, (self.psb[sb], self.psb[sb + 1]),
                         (psb_[base % NP], psb_[(base + 1) % NP]), scale=0.125)
            for idx in (base, base + 1):
                if 0 <= idx - LA < len(steps):
                    emit_pv(idx - LA)
        self.phase_end()


    def phase_m1_a(self, h_ext, g1, wqkv, w_bufs, qg, kg, cosT, sinT, valid, QTg, KTg, Vg, TOKn=TOK):
        NT = (TOKn + 2 * HALO) // 512
        tiles = [(i * 512, 4) for i in range(NT)]
        items = []
        for it in range(NT):
            for g in range(3):
                Dg = A_DIL[g]
                halo = 64 * Dg
                t0 = it * 512 - HALO
                lo, hi = max(t0, -halo), min(t0 + 512, TOKn + halo)
                if lo >= hi:
                    continue
                own = 0 <= t0 < TOKn
                for s_ in range(3):
                    if s_ == 0 and not own:
                        continue
                    items.append((it, g, s_, lo - t0, hi - t0))
        gains = self.f32(6)
        gb = Buf("gains")
        for g in range(3):
            self.load_gain(gains, g, qg[g, :], gb)
            self.load_gain(gains, 3 + g, kg[g, :], gb)
        vld = self.f32(4 * NT)
        vldb = Buf("valid")
        self.dma("sp", vld, valid, (), (vldb,))
        ones3 = self.bf(1024).rearrange("p (h n) -> p h n", n=128)
        ones3b = Buf("ones3")
        self.memset("pool", ones3, 1.0, (ones3b,))
        rotT, rotb = self.consts["rotA"]
        tabs = [self.f32(1024) for _ in range(3)]
        tabb = [Buf("tab") for _ in range(3)]
        NS = 3
        wsl = [self.bf(8192).rearrange("p (k n) -> p k n", n=1024) for _ in range(NS)]
        wslb = [[Buf("ws") for _ in range(2)] for _ in range(NS)]
        ot = [self.bf(4096).rearrange("p (c t) -> p c t", t=512) for _ in range(2)]
        otb = [Buf("ot") for _ in range(2)]
        vt = [self.bf(8192).rearrange("p (j h n) -> p j h n", j=4, h=8) for _ in range(2)]
        vtb = [Buf("vt") for _ in range(2)]
        scrs = [self.qk_scratch(4, 5), self.qk_scratch(4, 5), self.qk_scratch(4, 5)]
        ci = [0]
        rotv = [0]
        pipe = []

        def load_w(n):
            it, g, s_, c0, c1 = items[n]
            cb = g * 3 + s_
            for hf in range(2):
                self.dma("sp" if hf == 0 else "pool", wsl[n % NS][:, :, hf * 512:(hf + 1) * 512],
                         wqkv[:, cb * 1024 + hf * 512: cb * 1024 + (hf + 1) * 512].rearrange("(k p) n -> p k n", p=128),
                         tuple(w_bufs), (wslb[n % NS][hf],))

        def load_tab(i):
            self.dma("sp", tabs[i % 3][:, 0:512], cosT[:, i * 512:(i + 1) * 512], (), (tabb[i % 3],))
            self.dma("sp", tabs[i % 3][:, 512:1024], sinT[:, i * 512:(i + 1) * 512], (), (tabb[i % 3],))

        load_w(0)
        load_w(1)
        load_tab(0)
        gen = self.h_tiles(h_ext, tiles, g1, pst=(0,))
        cur = -1
        yT = yTb = None
        rot = [0]
        oi = [0]
        vi = [0]
        for n, (it, g, s_, c0, c1) in enumerate(items):
            if it != cur:
                cur, yT, yTb = next(gen)
                assert cur == it
                if it + 1 < NT:
                    load_tab(it + 1)
            if n + 2 < len(items):
                load_w(n + 2)
            tab, tb = tabs[it % 3], tabb[it % 3]
            ws, wb = wsl[n % NS], wslb[n % NS]
            Dg = A_DIL[g]
            N = c1 - c0
            t0 = it * 512 - HALO
            if s_ < 2:
                o, ob = ot[oi[0] % 2], otb[oi[0] % 2]
                oi[0] += 1
                nm = N // Dg
                if s_ == 0:
                    m0 = (t0 + c0) // Dg
                    dst = QTg[g]
                else:
                    m0 = (t0 + c0) // Dg + 64
                    dst = KTg[g]
                for c in range(8):
                    bk = 1 + rot[0] % 3
                    rot[0] += 1
                    scr = scrs[ci[0] % 3]
                    ci[0] += 1
                    for kc in range(8):
                        self.mm(self.ps[bk][:, 0:N], ws[:, kc, c * 128:(c + 1) * 128], yT[:, kc, c0:c1], kc == 0, kc == 7,
                                (wb[c // 4],) + tuple(yTb), (self.psb[bk],), signal=(kc == 7))
                    gcol = g if s_ == 0 else 3 + g
                    out_ap = o[:, c, 0:N].rearrange("p (r m) -> p m r", r=Dg)
                    iv = (lambda ap, Dg=Dg: ap.rearrange("p (m r) -> p m r", r=Dg))
                    job = {"s1": (lambda bk=bk, scr=scr, N=N: self.qk_s1(bk, N, scr)),
                           "s2": (lambda bk=bk, scr=scr, N=N, gcol=gcol: self.qk_s2(bk, N, gains[:, gcol:gcol + 1], gb, scr)),
                           "s3": (lambda scr=scr, N=N, tab=tab, tb=tb, c0=c0, c1=c1, out_ap=out_ap, ob=ob, iv=iv: self.qk_s3(
                               N, rotT, rotb, tab[:, c0:c1], tab[:, 512 + c0:512 + c1], tb, scr, out_ap, ob, in_view=iv))}
                    if c == 7:
                        def done(o=o, ob=ob, dst=dst, m0=m0, nm=nm, N=N, Dg=Dg):
                            for cc in range(8):
                                self.dma("sp" if cc % 2 == 0 else "pool", dst[cc * 128:(cc + 1) * 128, :, m0:m0 + nm],
                                         o[:, cc, 0:N].rearrange("p (r m) -> p r m", r=Dg), (ob,), ())
                        job["done"] = done
                    self.qk_pipe_push(pipe, job)
            else:
                v, vb = vt[vi[0] % 2], vtb[vi[0] % 2]
                vi[0] += 1
                js = [j for j in range(4) if j * 128 < c1 and (j + 1) * 128 > c0]
                for j in js:
                    for hf in range(2):
                        bk = 6 + rotv[0] % 2
                        rotv[0] += 1
                        for kc in range(8):
                            self.mm(self.ps[bk][:, :], yT[:, kc, j * 128:(j + 1) * 128], ws[:, kc, hf * 512:(hf + 1) * 512],
                                    kc == 0, kc == 7, (wb[hf],) + tuple(yTb), (self.psb[bk],), signal=(kc == 7))
                        outv = v.rearrange("p j h (x y) -> p j h x y", y=64)[:, j, 4 * hf:4 * hf + 4, 0:4:3, :]
                        self.copy("act", outv, self.ps[bk][:, :].rearrange("p (q a d) -> p q a d", a=2, d=64),
                                  (self.psb[bk],), (vb,))
                    self.ts("dve", v[:, j, :, 64:192], ones3, vld[:, it * 4 + j:it * 4 + j + 1], None, ALU.mult, None,
                            (ones3b, vldb), (vb,))
                    r0 = it * 512 + j * 128
                    self.dma("sp" if j % 2 == 0 else "pool", Vg[g][r0:r0 + 128, :, :], v[:, j, :, :], (vb,), ())
        self.qk_pipe_flush(pipe)
        self.phase_end()

    def phase_attn_a(self, QTg, KTg, Vg, aT, dbg=None, seg_off=0):
        mask, maskb = self.consts["maskA"]
        AccA = self.f32(4096)
        AccB = self.f32(4096)
        accb = [Buf("accA"), Buf("accB")]
        Acc = [AccA, AccB]
        tmpD = self.f32(4096)
        tmpDb = Buf("tmpD")
        ao = self.bf(4096)
        aob = Buf("ao")
        NSL = 3
        KTs = [self.bf(4224) for _ in range(NSL)]
        QTs = [self.bf(4096) for _ in range(NSL)]
        Vs = [self.bf(33 * 256).rearrange("p (i n) -> p i n", n=256) for _ in range(NSL)]
        slb = [[Buf("kts"), Buf("qts"), Buf("vs")] for _ in range(NSL)]
        NP = 4
        Pp = [self.bf(1024) for _ in range(NP // 2)]
        Ps = [Pp[i // 2][:, (i % 2) * 512:(i % 2 + 1) * 512] for i in range(NP)]
        Pb = [Buf("P") for _ in range(NP)]
        Pmp = [self.bf(1024) for _ in range(NP // 2)]
        Pm = [Pmp[i // 2][:, (i % 2) * 512:(i % 2 + 1) * 512] for i in range(NP)]
        Pmb = [Buf("Pm") for _ in range(NP)]
        mask2 = self.bf(1024)
        mask2b = Buf("mask2")
        self.copy("pool", mask2[:, 0:512], mask, (maskb,), (mask2b,))
        self.copy("pool", mask2[:, 512:1024], mask, (maskb,), (mask2b,))
        units = [(hp, g, r) for hp in range(8) for g in range(3) for r in range(A_DIL[g])]

        def load_unit(u):
            hp, g, r = units[u]
            Dg = A_DIL[g]
            Lc = TOK // Dg
            nkb = Lc // 128 + 1
            s_ = u % NSL
            so = seg_off // Dg
            self.dma("sp", KTs[s_][:, 0:Lc + 128], KTg[g][hp * 128:(hp + 1) * 128, r, so:so + Lc + 128], (), (slb[s_][0],))
            self.dma("pool", QTs[s_][:, 0:Lc], QTg[g][hp * 128:(hp + 1) * 128, r, so:so + Lc], (), (slb[s_][1],))
            R0 = HALO + seg_off - 64 * Dg + r
            n = 128 * nkb
            src = Vg[g][R0:R0 + (n - 1) * Dg + 1:Dg, hp, :].rearrange("(i k) n -> k i n", k=128)
            self.dma("sp", Vs[s_][:, 0:nkb, :], src, (), (slb[s_][2],))

        steps = []
        for u, (hp, g, r) in enumerate(units):
            Lc = TOK // A_DIL[g]
            for qc in range(Lc // 256):
                for hh in range(2):
                    steps.append((u, qc, hh))
        LA = 2
        load_unit(0)
        first_of_unit = {}
        last_of_hp = {}
        for i, (u, qc, hh) in enumerate(steps):
            first_of_unit.setdefault(u, i)
            last_of_hp[units[u][0]] = i
        def emit_qk(idx):
            u, qc, hh = steps[idx]
            hp, g, r = units[u]
            s_ = u % NSL
            if first_of_unit[u] == idx and u + 1 < len(units):
                load_unit(u + 1)
            Q0 = qc * 256
            pr = slice(hh * 64, (hh + 1) * 64)
            sb = idx % 4
            kt, qt = KTs[s_], QTs[s_]
            rd = (slb[s_][0], slb[s_][1])
            self.mm(self.ps[sb][:, 0:128], kt[pr, Q0:Q0 + 128], qt[pr, Q0:Q0 + 128], True, True, rd, (self.psb[sb],), signal=False)
            self.mm(self.ps[sb][:, 128:384], kt[pr, Q0 + 128:Q0 + 256], qt[pr, Q0:Q0 + 256], True, True, rd, (self.psb[sb],), signal=False)
            self.mm(self.ps[sb][:, 384:512], kt[pr, Q0 + 256:Q0 + 384], qt[pr, Q0 + 128:Q0 + 256], True, True, rd, (self.psb[sb],))

        def emit_exp(idx):
            sb = idx % 4
            self.act(Ps[idx % NP], self.ps[sb][:, :], AF.Exp, (self.psb[sb],), (Pb[idx % NP],), scale=0.125)
            self.tt("pool", Pm[idx % NP], Ps[idx % NP], mask, ALU.mult, (Pb[idx % NP], maskb), (Pmb[idx % NP],))

        def emit_pv(i2):
            u, qc, hh = steps[i2]
            hp, g, r = units[u]
            Dg = A_DIL[g]
            s_ = u % NSL
            ob = 4 + i2 % 4
            vs = Vs[s_]
            hs = slice(hh * 128, (hh + 1) * 128)
            pm = Pm[i2 % NP]
            rd = (slb[s_][2], Pmb[i2 % NP])
            self.mm(self.ps[ob][:, 0:256], vs[:, 2 * qc + 1, hs], pm[:, 128:384], True, False, rd, (self.psb[ob],), signal=False)
            self.mm(self.ps[ob][:, 0:128], vs[:, 2 * qc, hs], pm[:, 0:128], False, False, rd, (self.psb[ob],), signal=False)
            self.mm(self.ps[ob][:, 128:256], vs[:, 2 * qc + 2, hs], pm[:, 384:512], False, True, rd, (self.psb[ob],))
            Q0 = qc * 256
            accv = Acc[hh].rearrange("p (m r) -> p r m", r=Dg)[:, r, Q0:Q0 + 256]
            if g == 0:
                self.copy("dve", accv, self.ps[ob][:, 0:256], (self.psb[ob],), (accb[hh],))
            else:
                self.tt("dve", accv, self.ps[ob][:, 0:256], accv, ALU.add, (self.psb[ob], accb[hh]), (accb[hh],))
            if last_of_hp[hp] == i2:
                if dbg is not None and hp == 0:
                    self.dma("sp", dbg[0], AccA, (accb[0],), ())
                    self.dma("sp", dbg[1], AccB, (accb[1],), ())
                self.dma("sp", tmpD[0:64, :], AccA[64:128, :], (accb[0],), (tmpDb,))
                self.dma("sp", tmpD[64:128, :], AccB[0:64, :], (accb[1],), (tmpDb,))
                self.S.op("dve", lambda e, o=tmpD: e.reciprocal(o, o), (tmpDb,), (tmpDb,))
                self.tt("dve", ao[0:64, :], AccA[0:64, :], tmpD[0:64, :], ALU.mult, (accb[0], tmpDb), (aob,))
                self.tt("dve", ao[64:128, :], AccB[64:128, :], tmpD[64:128, :], ALU.mult, (accb[1], tmpDb), (aob,))
                self.dma("sp", aT[hp * 128:(hp + 1) * 128, seg_off:seg_off + TOK], ao, (aob,), ())

        for base in range(0, len(steps) + LA, 2):
            for idx in (base, base + 1):
                if idx < len(steps):
                    emit_qk(idx)
            if base + 1 < len(steps):
                sb = base % 4
                pi = (base % NP) // 2
                self.act(Pp[pi], self.ps2[sb // 2][:, :], AF.Exp, (self.psb[sb], self.psb[sb + 1]),
                         (Pb[base % NP], Pb[(base + 1) % NP]), scale=0.125)
                self.tt("pool", Pmp[pi], Pp[pi], mask2, ALU.mult, (Pb[base % NP], Pb[(base + 1) % NP], mask2b),
                        (Pmb[base % NP], Pmb[(base + 1) % NP]))
            for idx in (base, base + 1):
                if 0 <= idx - LA < len(steps):
                    emit_pv(idx - LA)
        self.phase_end()


    def phase_blend(self, flags, jobs):
        fl = self.f32(2)
        flb = Buf("flags")
        self.dma("sp", fl, flags, (), (flb,))
        ta = [self.f32(4096).rearrange("p (j d) -> p j d", d=1024) for _ in range(2)]
        tb = [self.f32(4096).rearrange("p (j d) -> p j d", d=1024) for _ in range(2)]
        tab_ = [Buf("ta") for _ in range(2)]
        tbb_ = [Buf("tb") for _ in range(2)]
        k = 0
        for dst, A, fa, B, fb in jobs:
            n = dst.shape[0]
            for c in range(n // 512):
                rs = slice(c * 512, (c + 1) * 512)
                a, ab = ta[k % 2], tab_[k % 2]
                b, bb = tb[k % 2], tbb_[k % 2]
                k += 1
                src0, f0 = (A, fa) if A is not None else (B, fb)
                self.dma("sp", a, src0[rs, :].rearrange("(j p) d -> p j d", p=128), (), (ab,))
                self.ts("dve", a, a, fl[:, f0:f0 + 1], None, ALU.mult, None, (ab, flb), (ab,))
                if A is not None and B is not None:
                    self.dma("pool", b, B[rs, :].rearrange("(j p) d -> p j d", p=128), (), (bb,))
                    self.stt("dve", a, b, fl[:, fb:fb + 1], a, ALU.mult, ALU.add, (bb, flb, ab), (ab,))
                self.dma("sp", dst[rs, :].rearrange("(j p) d -> p j d", p=128), a, (ab,), ())
        self.phase_end()

    def zero_rows(self, dsts):
        z = self.f32(4096).rearrange("p (j d) -> p j d", d=1024)
        zb = Buf("z")
        self.memset("dve", z, 0.0, (zb,))
        for dst in dsts:
            n = dst.shape[0]
            for c in range(n // 512):
                self.dma("sp", dst[c * 512:(c + 1) * 512, :].rearrange("(j p) d -> p j d", p=128), z, (zb,), ())
        self.phase_end()

def _consts_common():
    ident = np.eye(128, dtype=np.float32).astype(NPBF)
    return {"ident": ident}


def build_ffn_only():
    kb = KB()
    nc = kb.nc
    h_in = nc.dram_tensor("h_in", [TOK, DM], F32, kind="ExternalInput").ap()
    g2 = nc.dram_tensor("g2", [DM], F32, kind="ExternalInput").ap()
    w1 = nc.dram_tensor("w1", [DM, DFF], F32, kind="ExternalInput").ap()
    w2 = nc.dram_tensor("w2", [DFF, DM], F32, kind="ExternalInput").ap()
    ident = nc.dram_tensor("ident", [128, 128], BF16, kind="ExternalInput").ap()
    h_out = nc.dram_tensor("h_out", [TOK, DM], F32, kind="ExternalOutput").ap()
    kb.load_consts({"ident": (ident, BF16, 128)})
    w1b, w1bufs = kb.cvt(w1)
    w2b, w2bufs = kb.cvt(w2)
    dbg = {"yT": nc.dram_tensor("dbg_yT", [128, 4096], BF16, kind="ExternalOutput").ap(),
           "hid": nc.dram_tensor("dbg_hid", [128, 32 * 512], BF16, kind="ExternalOutput").ap(),
           "rr": nc.dram_tensor("dbg_rr", [128, 4], F32, kind="ExternalOutput").ap(),
           "ss": nc.dram_tensor("dbg_ss", [128, 4], F32, kind="ExternalOutput").ap()}
    kb.phase_out_ffn(h_in, 0, None, None, None, g2, w1b, w1bufs, w2b, w2bufs, h_out, dbg=dbg)
    with nc.Block() as block:
        kb.S.emit(block)
    kb.es.close()
    return nc


def _declare(nc, name, shape, dt, kind="ExternalInput"):
    return nc.dram_tensor(name, list(shape), dt, kind=kind).ap()


def build_layer_b():
    kb = KB()
    nc = kb.nc
    h_ext = _declare(nc, "h_ext", [EXT, DM], F32)
    g1 = _declare(nc, "g1", [DM], F32)
    g2 = _declare(nc, "g2", [DM], F32)
    wa = _declare(nc, "wa", [DM, 3072], F32)
    wo = _declare(nc, "wo", [DM, DM], F32)
    conv = _declare(nc, "conv", [3, DM], F32)
    w1 = _declare(nc, "w1", [DM, DFF], F32)
    w2 = _declare(nc, "w2", [DFF, DM], F32)
    ident = _declare(nc, "ident", [128, 128], BF16)
    h_out = _declare(nc, "h_out", [TOK, DM], F32, kind="ExternalOutput")
    kb.load_consts({"ident": (ident, BF16, 128)})
    wab, wabufs = kb.cvt(wa)
    wob, wobufs = kb.cvt(wo)
    w1b, w1bufs = kb.cvt(w1)
    w2b, w2bufs = kb.cvt(w2)
    uT = kb.dram([DM, TOK + 256], BF16)
    bT = kb.dram([DM, TOK], BF16)
    zT = kb.dram([DM, TOK], BF16)
    kb.phase_m1_b(h_ext, g1, wab, wabufs, uT, bT)
    kb.phase_conv_b(uT, bT, conv, zT)
    kb.phase_out_ffn(h_ext, HALO, zT, wob, wobufs, g2, w1b, w1bufs, w2b, w2bufs, h_out)
    with nc.Block() as block:
        kb.S.emit(block)
    kb.es.close()
    return nc


def rope_tables_c(pos):
    inv = (10000.0 ** (-np.arange(0, 32, 2, dtype=np.float32) / np.float32(32))).astype(np.float32)
    row = (pos // 64).astype(np.float32)
    col = (pos % 64).astype(np.float32)
    ang_r = row[None, :] * inv[:, None]
    ang_c = col[None, :] * inv[:, None]
    cos64 = np.concatenate([np.cos(ang_r), np.cos(ang_r), np.cos(ang_c), np.cos(ang_c)], 0)
    sin64 = np.concatenate([np.sin(ang_r), np.sin(ang_r), np.sin(ang_c), np.sin(ang_c)], 0)
    return (np.concatenate([cos64, cos64], 0).astype(np.float32), np.concatenate([sin64, sin64], 0).astype(np.float32))


def rot_matrix(kind):
    R = np.zeros((128, 128), np.float32)
    for b in (0, 64):
        if kind == "C":
            for off in (0, 32):
                for i in range(16):
                    R[b + off + i + 16, b + off + i] = -1.0
                    R[b + off + i, b + off + i + 16] = 1.0
        else:
            for i in range(8):
                R[b + i + 8, b + i] = -1.0
                R[b + i, b + i + 8] = 1.0
    return R.astype(NPBF)


def onesbd_matrix():
    M = np.zeros((128, 128), np.float32)
    M[0:64, 0:64] = 1.0
    M[64:128, 64:128] = 1.0
    return M.astype(NPBF)


def build_layer_c():
    kb = KB()
    nc = kb.nc
    h_ext = _declare(nc, "h_ext", [SEQ, DM], F32)
    g1 = _declare(nc, "g1", [DM], F32)
    g2 = _declare(nc, "g2", [DM], F32)
    wa = _declare(nc, "wa", [DM, 1536], F32)
    wo = _declare(nc, "wo", [DM, DM], F32)
    qg = _declare(nc, "qg", [64], F32)
    kg = _declare(nc, "kg", [64], F32)
    w1 = _declare(nc, "w1", [DM, DFF], F32)
    w2 = _declare(nc, "w2", [DFF, DM], F32)
    ident = _declare(nc, "ident", [128, 128], BF16)
    onesbd = _declare(nc, "onesbd", [128, 128], BF16)
    rotC = _declare(nc, "rotC", [128, 128], BF16)
    cosT = _declare(nc, "cosT", [128, SEQ], F32)
    sinT = _declare(nc, "sinT", [128, SEQ], F32)
    h_out = _declare(nc, "h_out", [TOK, DM], F32, kind="ExternalOutput")
    kb.load_consts({"ident": (ident, BF16, 128), "onesbd": (onesbd, BF16, 128), "rotC": (rotC, BF16, 128)})
    wab, wabufs = kb.cvt(wa)
    wob, wobufs = kb.cvt(wo)
    w1b, w1bufs = kb.cvt(w1)
    w2b, w2bufs = kb.cvt(w2)
    QT = kb.dram([DM, TOK], BF16)
    KT = kb.dram([256, SEQ], BF16)
    V = kb.dram([SEQ, 256], BF16)
    aT = kb.dram([DM, TOK], BF16)
    kb.phase_m1_c(h_ext, g1, wab, wabufs, qg, kg, cosT, sinT, QT, KT, V)
    kb.phase_attn_c(QT, KT, V, aT)
    kb.phase_out_ffn(h_ext, 0, aT, wob, wobufs, g2, w1b, w1bufs, w2b, w2bufs, h_out)
    with nc.Block() as block:
        kb.S.emit(block)
    kb.es.close()
    return nc


def rope_tables_a(pos):
    inv = (500000.0 ** (-np.arange(0, 16, 2, dtype=np.float32) / np.float32(16))).astype(np.float32)
    ang = pos.astype(np.float32)[None, :] * inv[:, None]
    n = pos.shape[0]
    cos64 = np.ones((64, n), np.float32)
    sin64 = np.zeros((64, n), np.float32)
    cos64[0:8] = np.cos(ang)
    cos64[8:16] = np.cos(ang)
    sin64[0:8] = np.sin(ang)
    sin64[8:16] = np.sin(ang)
    return np.concatenate([cos64, cos64], 0), np.concatenate([sin64, sin64], 0)


def band_mask():
    k = np.arange(128)[:, None]
    m = np.zeros((128, 512), np.float32)
    j = np.arange(128)[None, :]
    m[:, 0:128] = (j <= k)
    j2 = np.arange(256)[None, :]
    m[:, 128:384] = (j2 >= k) & (j2 <= k + 128)
    m[:, 384:512] = (j >= k)
    return m.astype(NPBF)


def build_layer_a(stop=3, debug=False):
    kb = KB()
    dk = 'ExternalOutput' if debug else 'Internal'
    nc = kb.nc
    h_ext = _declare(nc, "h_ext", [EXT, DM], F32)
    g1 = _declare(nc, "g1", [DM], F32)
    g2 = _declare(nc, "g2", [DM], F32)
    wa = _declare(nc, "wa", [DM, 9216], F32)
    wo = _declare(nc, "wo", [DM, DM], F32)
    qg = _declare(nc, "qg", [3, 64], F32)
    kg = _declare(nc, "kg", [3, 64], F32)
    w1 = _declare(nc, "w1", [DM, DFF], F32)
    w2 = _declare(nc, "w2", [DFF, DM], F32)
    ident = _declare(nc, "ident", [128, 128], BF16)
    onesbd = _declare(nc, "onesbd", [128, 128], BF16)
    rotA = _declare(nc, "rotA", [128, 128], BF16)
    maskA = _declare(nc, "maskA", [128, 512], BF16)
    cosT = _declare(nc, "cosT", [128, EXT], F32)
    sinT = _declare(nc, "sinT", [128, EXT], F32)
    valid = _declare(nc, "valid", [128, EXT // 128], F32)
    h_out = _declare(nc, "h_out", [TOK, DM], F32, kind="ExternalOutput")
    kb.load_consts({"ident": (ident, BF16, 128), "onesbd": (onesbd, BF16, 128), "rotA": (rotA, BF16, 128),
                    "maskA": (maskA, BF16, 512)})
    wab, wabufs = kb.cvt(wa)
    wob, wobufs = kb.cvt(wo)
    w1b, w1bufs = kb.cvt(w1)
    w2b, w2bufs = kb.cvt(w2)
    QTg = [kb.dram([DM, A_DIL[g], TOK // A_DIL[g]], BF16, name="dbg_qt%d" % g, kind=dk) for g in range(3)]
    KTg = [kb.dram([DM, A_DIL[g], TOK // A_DIL[g] + 128], BF16, name="dbg_kt%d" % g, kind=dk) for g in range(3)]
    Vg = [kb.dram([EXT, 8, 256], BF16, name="dbg_v%d" % g, kind=dk) for g in range(3)]
    aT = kb.dram([DM, TOK], BF16, name="dbg_aT", kind=dk)
    kb.phase_m1_a(h_ext, g1, wab, wabufs, qg, kg, cosT, sinT, valid, QTg, KTg, Vg)
    if stop >= 2:
        dbg = None
        if debug:
            dbg = [kb.dram([128, TOK], F32, name="dbg_accA", kind=dk), kb.dram([128, TOK], F32, name="dbg_accB", kind=dk)]
        kb.phase_attn_a(QTg, KTg, Vg, aT, dbg)
    if stop >= 3:
        kb.phase_out_ffn(h_ext, HALO, aT if stop >= 2 else None, wob, wobufs, g2, w1b, w1bufs, w2b, w2bufs, h_out)
    with nc.Block() as block:
        kb.S.emit(block)
    kb.es.close()
    return nc


SEQX = SEQ + 2 * HALO
NQ2 = 5120


def build_fused():
    kb = KB()
    nc = kb.nc
    D_ = _declare
    x_ext = D_(nc, "x_ext", [SEQX, DM], F32)
    g1 = D_(nc, "g1", [4, DM], F32)
    g2 = D_(nc, "g2", [4, DM], F32)
    a_wqkv = [D_(nc, "a_wqkv%d" % i, [DM, 9216], F32) for i in range(2)]
    a_wo = [D_(nc, "a_wo%d" % i, [DM, DM], F32) for i in range(2)]
    a_qg = [D_(nc, "a_qg%d" % i, [3, 64], F32) for i in range(2)]
    a_kg = [D_(nc, "a_kg%d" % i, [3, 64], F32) for i in range(2)]
    b_win = D_(nc, "b_win", [DM, 3072], F32)
    b_wout = D_(nc, "b_wout", [DM, DM], F32)
    b_conv = D_(nc, "b_conv", [3, DM], F32)
    c_wqkv = D_(nc, "c_wqkv", [DM, 1536], F32)
    c_wo = D_(nc, "c_wo", [DM, DM], F32)
    c_qg = D_(nc, "c_qg", [64], F32)
    c_kg = D_(nc, "c_kg", [64], F32)
    w1 = [D_(nc, "w1_%d" % i, [DM, DFF], F32) for i in range(4)]
    w2 = [D_(nc, "w2_%d" % i, [DFF, DM], F32) for i in range(4)]
    ident = D_(nc, "ident", [128, 128], BF16)
    onesbd = D_(nc, "onesbd", [128, 128], BF16)
    rotA = D_(nc, "rotA", [128, 128], BF16)
    rotC = D_(nc, "rotC", [128, 128], BF16)
    maskA = D_(nc, "maskA", [128, 512], BF16)
    cosA0 = D_(nc, "cosA0", [128, SEQX], F32)
    sinA0 = D_(nc, "sinA0", [128, SEQX], F32)
    validA0 = D_(nc, "validA0", [128, SEQX // 128], F32)
    cosA3 = D_(nc, "cosA3", [128, EXT], F32)
    sinA3 = D_(nc, "sinA3", [128, EXT], F32)
    validA3 = D_(nc, "validA3", [128, EXT // 128], F32)
    cosC = D_(nc, "cosC", [128, SEQ], F32)
    sinC = D_(nc, "sinC", [128, SEQ], F32)
    flags = D_(nc, "flags", [128, 2], F32)
    h_out = D_(nc, "h_out", [TOK, DM], F32, kind="ExternalOutput")

    kb.load_consts({"ident": (ident, BF16, 128), "onesbd": (onesbd, BF16, 128), "rotA": (rotA, BF16, 128),
                    "rotC": (rotC, BF16, 128), "maskA": (maskA, BF16, 512)})

    wq0 = kb.cvt(a_wqkv[0])
    wo0 = kb.cvt(a_wo[0])
    f1_0 = kb.cvt(w1[0])
    f2_0 = kb.cvt(w2[0])
    h1_ext = kb.dram([SEQX, DM], F32)
    kb.zero_rows([h1_ext[0:HALO, :], h1_ext[HALO + SEQ:SEQX, :]])
    QTg0 = [kb.dram([DM, A_DIL[g], SEQ // A_DIL[g]], BF16) for g in range(3)]
    KTg0 = [kb.dram([DM, A_DIL[g], SEQ // A_DIL[g] + 128], BF16) for g in range(3)]
    Vg0 = [kb.dram([SEQX, 8, 256], BF16) for g in range(3)]
    aT0 = kb.dram([DM, SEQ], BF16)
    kb.phase_m1_a(x_ext, g1[0, :], wq0[0], wq0[1], a_qg[0], a_kg[0], cosA0, sinA0, validA0, QTg0, KTg0, Vg0, TOKn=SEQ)
    bw = kb.cvt(b_win)
    bo = kb.cvt(b_wout)
    f1_1 = kb.cvt(w1[1])
    f2_1 = kb.cvt(w2[1])
    for seg in range(2):
        kb.phase_attn_a(QTg0, KTg0, Vg0, aT0, None, seg_off=seg * TOK)
    kb.phase_out_ffn(x_ext, HALO, aT0, wo0[0], wo0[1], g2[0, :], f1_0[0], f1_0[1], f2_0[0], f2_0[1], h1_ext,
                     ntile=16, out_row0=HALO)

    uT = kb.dram([DM, SEQ + 256], BF16)
    bT = kb.dram([DM, SEQ], BF16)
    zT = kb.dram([DM, SEQ], BF16)
    h2 = kb.dram([SEQ, DM], F32)
    kb.phase_m1_b(h1_ext, g1[1, :], bw[0], bw[1], uT, bT, ntile=16)
    cw = kb.cvt(c_wqkv)
    co = kb.cvt(c_wo)
    f1_2 = kb.cvt(w1[2])
    f2_2 = kb.cvt(w2[2])
    kb.phase_conv_b(uT, bT, b_conv, zT, ntile=16)
    kb.phase_out_ffn(h1_ext, HALO, zT, bo[0], bo[1], g2[1, :], f1_1[0], f1_1[1], f2_1[0], f2_1[1], h2, ntile=16)

    h2c = kb.dram([SEQ, DM], F32)
    kb.phase_blend(flags, [(h2c[0:NQ2, :], h2[0:NQ2, :], 0, h2[SEQ - NQ2:SEQ, :], 1),
                           (h2c[NQ2:SEQ, :], h2[NQ2:SEQ, :], 0, h2[0:SEQ - NQ2, :], 1)])
    QT = kb.dram([DM, NQ2], BF16)
    KT = kb.dram([256, SEQ], BF16)
    V = kb.dram([SEQ, 256], BF16)
    aT2 = kb.dram([DM, NQ2], BF16)
    h3c = kb.dram([NQ2, DM], F32)
    kb.phase_m1_c(h2c, g1[2, :], cw[0], cw[1], c_qg, c_kg, cosC, sinC, QT, KT, V, n_own=NQ2 // 512)
    wq3 = kb.cvt(a_wqkv[1])
    wo3 = kb.cvt(a_wo[1])
    f1_3 = kb.cvt(w1[3])
    f2_3 = kb.cvt(w2[3])
    kb.phase_attn_c(QT, KT, V, aT2, n_q=NQ2 // 512)
    kb.phase_out_ffn(h2c, 0, aT2, co[0], co[1], g2[2, :], f1_2[0], f1_2[1], f2_2[0], f2_2[1], h3c, ntile=NQ2 // 512)

    h3_ext = kb.dram([EXT, DM], F32)
    kb.phase_blend(flags, [(h3_ext[0:HALO, :], None, 0, h3c[0:HALO, :], 1),
                           (h3_ext[HALO:HALO + TOK, :], h3c[0:TOK, :], 0, h3c[HALO:HALO + TOK, :], 1),
                           (h3_ext[HALO + TOK:EXT, :], h3c[TOK:NQ2, :], 0, None, 1)])
    QTg3 = [kb.dram([DM, A_DIL[g], TOK // A_DIL[g]], BF16) for g in range(3)]
    KTg3 = [kb.dram([DM, A_DIL[g], TOK // A_DIL[g] + 128], BF16) for g in range(3)]
    Vg3 = [kb.dram([EXT, 8, 256], BF16) for g in range(3)]
    aT3 = kb.dram([DM, TOK], BF16)
    kb.phase_m1_a(h3_ext, g1[3, :], wq3[0], wq3[1], a_qg[1], a_kg[1], cosA3, sinA3, validA3, QTg3, KTg3, Vg3, TOKn=TOK)
    kb.phase_attn_a(QTg3, KTg3, Vg3, aT3, None, seg_off=0)
    kb.phase_out_ffn(h3_ext, HALO, aT3, wo3[0], wo3[1], g2[3, :], f1_3[0], f1_3[1], f2_3[0], f2_3[1], h_out)
    with nc.Block() as block:
        kb.S.emit(block)
    kb.es.close()
    return nc


_PROG = {}


def kernel(x, norm1, norm2, a_wqkv, a_q_gain, a_k_gain, a_wo, b_win, b_conv, b_wout,
           c_wqkv, c_q_gain, c_k_gain, c_wo, mlp_w1, mlp_w2):
    f = lambda a: np.ascontiguousarray(np.asarray(a, dtype=np.float32))
    x = f(x)
    if "nc" not in _PROG:
        _PROG["nc"] = build_fused()
    nc = _PROG["nc"]
    common = {
        "g1": f(norm1), "g2": f(norm2),
        "a_wqkv0": f(a_wqkv[0]), "a_wqkv1": f(a_wqkv[1]), "a_wo0": f(a_wo[0]), "a_wo1": f(a_wo[1]),
        "a_qg0": f(a_q_gain[0]), "a_qg1": f(a_q_gain[1]), "a_kg0": f(a_k_gain[0]), "a_kg1": f(a_k_gain[1]),
        "b_win": f(b_win[0]), "b_wout": f(b_wout[0]), "b_conv": f(b_conv[0]),
        "c_wqkv": f(c_wqkv[0]), "c_wo": f(c_wo[0]), "c_qg": f(c_q_gain[0]), "c_kg": f(c_k_gain[0]),
        "ident": np.eye(128, dtype=np.float32).astype(NPBF), "onesbd": onesbd_matrix(),
        "rotA": rot_matrix("A"), "rotC": rot_matrix("C"), "maskA": band_mask(),
    }
    for i in range(4):
        common["w1_%d" % i] = f(mlp_w1[i])
        common["w2_%d" % i] = f(mlp_w2[i])
    pos0 = np.arange(-HALO, SEQ + HALO)
    cosA0, sinA0 = rope_tables_a(pos0)
    valid0 = ((pos0 >= 0) & (pos0 < SEQ)).astype(np.float32)
    common.update({"cosA0": cosA0, "sinA0": sinA0,
                   "validA0": np.ascontiguousarray(valid0.reshape(SEQX // 128, 128).T)})
    in_maps = []
    for core in range(NCORES):
        b, half = core // 2, core % 2
        m = dict(common)
        x_ext = np.zeros((SEQX, DM), np.float32)
        x_ext[HALO:HALO + SEQ] = x[b]
        pos3 = np.arange(half * TOK - HALO, half * TOK - HALO + EXT)
        cosA3, sinA3 = rope_tables_a(pos3)
        valid3 = ((pos3 >= 0) & (pos3 < SEQ)).astype(np.float32)
        order = (np.arange(SEQ) + (SEQ - NQ2) * half) % SEQ
        cosC, sinC = rope_tables_c(order)
        fl = np.zeros((128, 2), np.float32)
        fl[:, half] = 1.0
        m.update({"x_ext": x_ext, "cosA3": cosA3, "sinA3": sinA3,
                  "validA3": np.ascontiguousarray(valid3.reshape(EXT // 128, 128).T),
                  "cosC": cosC, "sinC": sinC, "flags": fl})
        in_maps.append(m)
    res = run_bass_kernel_spmd(nc, in_maps, core_ids=list(range(NCORES)))
    out = np.empty((NB, SEQ, DM), np.float32)
    for core in range(NCORES):
        b, half = core // 2, core % 2
        out[b, half * TOK:(half + 1) * TOK] = res.results[core]["h_out"]
    return out
```

```python
import contextlib
import numpy as np
import ml_dtypes
import concourse.bass as bass
import concourse.mybir as mybir
from concourse.bass_utils import run_bass_kernel_spmd

F32 = mybir.dt.float32
BF16 = mybir.dt.bfloat16
AF = mybir.ActivationFunctionType
ALU = mybir.AluOpType
NPBF = ml_dtypes.bfloat16

DM = 1024
SEQ = 8192
NB = 4
NCORES = 8
TOK = 4096
HALO = 1024
EXT = TOK + 2 * HALO
EPS = 1e-6
DFF = 4096
A_DIL = (1, 4, 16)


class Buf:
    __slots__ = ("w", "r", "name")

    def __init__(self, name=""):
        self.w = {}
        self.r = {}
        self.name = name


class Sched:
    CE = ("pe", "act", "dve", "pool")

    def __init__(self, nc, es, ndq=14):
        self.nc = nc
        self.q = {e: [] for e in ("pe", "act", "dve", "pool", "sp")}
        self.sem = {}
        self.cnt = {}
        for e in self.CE:
            self.sem[e] = es.enter_context(nc.semaphore("s_" + e))
            self.cnt[e] = 0
        self.dq = {}
        for qn in ("sp", "pool", "act"):
            ks = []
            for i in range(ndq):
                k = "d%s%d" % (qn, i)
                self.sem[k] = es.enter_context(nc.semaphore(k))
                self.cnt[k] = 0
                ks.append(k)
            self.dq[qn] = [ks, 0]
        self.seen = {e: {} for e in self.q}
        self.nwaits = 0

    def _need(self, eng, reads, writes):
        need = {}
        for b in reads:
            for k, v in b.w.items():
                if need.get(k, 0) < v:
                    need[k] = v
        for b in writes:
            for k, v in b.w.items():
                if need.get(k, 0) < v:
                    need[k] = v
            for k, v in b.r.items():
                if need.get(k, 0) < v:
                    need[k] = v
        waits = []
        seen = self.seen[eng]
        for k, v in need.items():
            if k == "pe" and eng == "pe":
                continue
            if seen.get(k, 0) >= v:
                continue
            seen[k] = v
            waits.append((k, v))
        self.nwaits += len(waits)
        return waits

    def op(self, eng, emit, reads=(), writes=(), signal=True):
        waits = self._need(eng, reads, writes)
        if signal:
            self.cnt[eng] += 1
            v = self.cnt[eng]
        else:
            v = self.cnt[eng] + 1
        for b in reads:
            if b.r.get(eng, 0) < v:
                b.r[eng] = v
        for b in writes:
            b.w = {eng: v}
            b.r = {}
        self.q[eng].append((waits, emit, eng if signal else None, 1))

    def dma(self, qn, out, in_, reads=(), writes=(), slow=False):
        waits = self._need(qn, reads, writes)
        ks, i = self.dq[qn]
        k = ks[i % len(ks)]
        self.dq[qn][1] = i + 1
        if self.cnt[k] > self.seen[qn].get(k, 0):
            self.seen[qn][k] = self.cnt[k]
            waits.append((k, self.cnt[k]))
        self.cnt[k] += 16
        v = self.cnt[k]
        for b in reads:
            if b.r.get(k, 0) < v:
                b.r[k] = v
        for b in writes:
            b.w = {k: v}
            b.r = {}
        if slow:
            self.q[qn].append((waits, (lambda e, o=out, i_=in_: e.dma_start(out=o, in_=i_, allow_slow_non_contiguous=True)), k, 16))
        else:
            self.q[qn].append((waits, (lambda e, o=out, i_=in_: e.dma_start(out=o, in_=i_)), k, 16))

    def barrier(self):
        for e in self.q:
            waits = []
            for k, v in self.cnt.items():
                if k == "pe" and e == "pe":
                    continue
                if v > self.seen[e].get(k, 0):
                    self.seen[e][k] = v
                    waits.append((k, v))
            if waits:
                self.q[e].append((waits, None, None, 0))

    def emit(self, block):
        def run(name):
            def f(e):
                for waits, emit, k, inc in self.q[name]:
                    for wk, wv in waits:
                        e.wait_ge(self.sem[wk], wv)
                    if emit is not None:
                        ins = emit(e)
                        if k is not None:
                            ins.then_inc(self.sem[k], inc)
            return f

        block.tensor(run("pe"))
        block.scalar(run("act"))
        block.vector(run("dve"))
        block.gpsimd(run("pool"))
        block.sync(run("sp"))


ARENA_WORDS = 50000


class KB:
    def __init__(self):
        self.nc = bass.Bass("TRN2", target_bir_lowering=False)
        self.es = contextlib.ExitStack()
        nc = self.nc
        self.S = Sched(nc, self.es)
        self.arena = self.es.enter_context(nc.sbuf_tensor("arena", [128, ARENA_WORDS], F32))
        self.ps2 = [self.es.enter_context(nc.psum_tensor("ps%d" % i, [128, 1024], F32)) for i in range(4)]
        self.ps = [self.ps2[i // 2][:, (i % 2) * 512:(i % 2 + 1) * 512] for i in range(8)]
        self.psb = [Buf("ps%d" % i) for i in range(8)]
        self.off = 0
        self.base = 0
        self.dram_n = 0
        self.consts = {}

    def f32(self, n):
        assert self.off + n <= ARENA_WORDS, ("sbuf overflow", self.off, n)
        ap = self.arena[:, self.off:self.off + n]
        self.off += n
        return ap

    def bf(self, n):
        w = (n + 1) // 2
        return self.f32(w).bitcast(BF16)[:, 0:n]

    def phase_end(self):
        self.S.barrier()
        self.off = self.base

    def dram(self, shape, dt, name=None, kind="Internal"):
        self.dram_n += 1
        nm = name or ("scr%d" % self.dram_n)
        return self.nc.dram_tensor(nm, list(shape), dt, kind=kind).ap()

    def mm(self, out, lhsT, rhs, start, stop, reads, writes, signal=True):
        self.S.op("pe", lambda e: e.matmul(out, lhsT, rhs, start=start, stop=stop, skip_group_check=True),
                  reads, writes, signal)

    def tr(self, out, in_, ident, reads, writes, signal=True):
        self.S.op("pe", lambda e: e.transpose(out, in_, ident), reads, writes, signal)

    def act(self, out, in_, func, reads, writes, bias=None, scale=None, accum_out=None):
        kw = {}
        if bias is not None:
            kw["bias"] = bias
        if scale is not None:
            kw["scale"] = scale
        if accum_out is not None:
            kw["accum_out"] = accum_out
        self.S.op("act", lambda e: e.activation(out, in_, func, **kw), reads, writes)

    def tt(self, eng, out, in0, in1, op, reads, writes):
        self.S.op(eng, lambda e: e.tensor_tensor(out, in0, in1, op), reads, writes)

    def ts(self, eng, out, in0, s1, s2, op0, op1, reads, writes):
        if s2 is None:
            self.S.op(eng, lambda e: e.tensor_scalar(out, in0, s1, None, op0), reads, writes)
        else:
            self.S.op(eng, lambda e: e.tensor_scalar(out, in0, s1, s2, op0, op1), reads, writes)

    def stt(self, eng, out, in0, scalar, in1, op0, op1, reads, writes):
        self.S.op(eng, lambda e: e.scalar_tensor_tensor(out, in0, scalar, in1, op0, op1), reads, writes)

    def copy(self, eng, out, in_, reads, writes):
        if eng == "act":
            self.S.op("act", lambda e: e.copy(out, in_), reads, writes)
        else:
            self.S.op(eng, lambda e: e.tensor_copy(out, in_), reads, writes)

    def memset(self, eng, out, val, writes):
        self.S.op(eng, lambda e: e.memset(out, val), (), writes)

    def dma(self, q, out, in_, reads=(), writes=(), slow=False):
        self.S.dma(q, out, in_, reads, writes, slow)

    def cvt(self, src, rows_per=None):
        K, N = src.shape
        dst = self.dram([K, N], BF16)
        if rows_per is None:
            rows_per = 16
            while rows_per * 2 <= min(K, (1 << 20) // N) and K % (rows_per * 2) == 0:
                rows_per *= 2
        bufs = []
        for r0 in range(0, K, rows_per):
            b = Buf("cvt")
            self.dma("pool", dst[r0:r0 + rows_per, :], src[r0:r0 + rows_per, :], (), (b,))
            bufs.append(b)
        return dst, bufs

    def load_consts(self, cd):
        for name, (ap, dt, cols) in cd.items():
            t = self.bf(cols) if dt == BF16 else self.f32(cols)
            b = Buf(name)
            self.dma("sp", t, ap, (), (b,))
            self.consts[name] = (t, b)
        t = self.f32(2)
        b = Buf("eps")
        self.memset("dve", t[:, 0:1], EPS, (b,))
        self.memset("dve", t[:, 1:2], 64 * EPS, (b,))
        self.consts["eps"] = (t, b)
        self.base = self.off

    def norm_T(self, ht, hb, nsub, g32, g32b, yT, yTb, scr):
        ident, identb = self.consts["ident"]
        ss, ssb, rr, rrb, ytok, ytokb, pst = scr
        for j in range(nsub):
            self.act(ytok[:, j, :], ht[:, j, :], AF.Square, (hb,), (ytokb[j], ssb[j]), accum_out=ss[:, j:j + 1])
            self.act(rr[:, j:j + 1], ss[:, j:j + 1], AF.Sqrt, (ssb[j],), (rrb[j],), bias=self.consts["eps"][0][:, 0:1],
                     scale=1.0 / DM)
            self.S.op("dve", lambda e, o=rr[:, j:j + 1]: e.reciprocal(o, o), (rrb[j],), (rrb[j],))
            self.stt("dve", ytok[:, j, :], ht[:, j, :], rr[:, j:j + 1], g32, ALU.mult, ALU.mult,
                     (hb, rrb[j], g32b), (ytokb[j],))
            bank = pst[j % len(pst)]
            psv = self.ps[bank][:].bitcast(BF16).rearrange("p (c t) -> p c t", t=128)
            for c in range(8):
                self.tr(psv[:, c, :], ytok[:, j, c * 128:(c + 1) * 128], ident,
                        (ytokb[j], identb), (self.psb[bank],), signal=(c == 7))
            self.copy("act" if j % 2 == 0 else "dve", yT[:, :, j * 128:(j + 1) * 128], psv,
                      (self.psb[bank],), (yTb[j],))

    def norm_scratch(self, pst):
        ss = self.f32(4)
        rr = self.f32(4)
        ytok = self.bf(4 * 1024).rearrange("p (j d) -> p j d", d=1024)
        return (ss, [Buf("ss") for _ in range(4)], rr, [Buf("rr") for _ in range(4)],
                ytok, [Buf("ytok") for _ in range(4)], pst)

    def load_g32(self, g_ap):
        g = self.f32(1024)
        gb = Buf("g")
        self.dma("sp", g, g_ap.partition_broadcast(128), (), (gb,))
        return g, gb

    def phase_out_ffn(self, h_in, row0, aT, wo, wo_bufs, g2, w1, w1_bufs, w2, w2_bufs, h_out, dbg=None, ntile=8, out_row0=0):
        S = self.S
        g32, g32b = self.load_g32(g2)
        wos = None
        if aT is not None:
            wos = self.bf(8 * 1024).rearrange("p (k n) -> p k n", n=1024)
            wosb = Buf("wo")
            self.dma("sp", wos, wo.rearrange("(k p) n -> p k n", p=128), tuple(wo_bufs), (wosb,))
        NSLOT = 2
        wsl = [self.bf(16384) for _ in range(NSLOT)]
        wslb = [[Buf("ws") for _ in range(4)] for _ in range(NSLOT)]
        hts = [self.f32(4096).rearrange("p (j d) -> p j d", d=1024) for _ in range(2)]
        htb = [Buf("ht") for _ in range(2)]
        yT = self.bf(4096).rearrange("p (c t) -> p c t", t=512)
        yTb = [Buf("yT") for _ in range(4)]
        hid = self.bf(32 * 512).rearrange("p (m t) -> p m t", t=512)
        hidb = [Buf("hid") for _ in range(32)]
        relu = [self.f32(512) for _ in range(2)]
        relub = [Buf("relu") for _ in range(2)]
        if aT is not None:
            ats = [self.bf(4096).rearrange("p (c t) -> p c t", t=512) for _ in range(2)]
            atb = [Buf("at") for _ in range(2)]
        scr = self.norm_scratch([0, 1])
        slot_i = [0]

        def load_w(kind, idx):
            s = slot_i[0] % NSLOT
            slot_i[0] += 1
            if kind == "w1":
                v = wsl[s].rearrange("p (q k n) -> p q k n", q=4, k=8)
                for qd in range(4):
                    self.dma("sp" if qd % 2 == 0 else "pool", v[:, qd, :, :],
                             w1[:, idx * 2048 + qd * 512: idx * 2048 + (qd + 1) * 512].rearrange("(k p) n -> p k n", p=128),
                             tuple(w1_bufs), (wslb[s][qd],))
            else:
                v = wsl[s].rearrange("p (k n) -> p k n", n=512)
                for qd in range(4):
                    self.dma("sp" if qd % 2 == 0 else "pool", v[:, qd * 8:(qd + 1) * 8, :],
                             w2[qd * 1024:(qd + 1) * 1024, idx * 512:(idx + 1) * 512].rearrange("(k p) n -> p k n", p=128),
                             tuple(w2_bufs), (wslb[s][qd],))
            return v, wslb[s]

        def load_h(i):
            r = row0 + i * 512
            self.dma("sp", hts[i % 2], h_in[r:r + 512, :].rearrange("(j p) d -> p j d", p=128), (), (htb[i % 2],))
            if aT is not None:
                self.dma("sp", ats[i % 2], aT[:, i * 512:(i + 1) * 512].rearrange("(c p) t -> p c t", p=128),
                         (), (atb[i % 2],))

        load_h(0)
        pending_w = [load_w("w1", 0)]
        psrot = [0]

        def nbank():
            b = 2 + psrot[0] % 6
            psrot[0] += 1
            return b

        for i in range(ntile):
            ht, hb = hts[i % 2], htb[i % 2]
            if i + 1 < ntile:
                load_h(i + 1)
            if aT is not None:
                at, ab = ats[i % 2], atb[i % 2]
                for j in range(4):
                    for f in range(2):
                        bk = nbank()
                        for kc in range(8):
                            self.mm(self.ps[bk][:, :], at[:, kc, j * 128:(j + 1) * 128], wos[:, kc, f * 512:(f + 1) * 512],
                                    kc == 0, kc == 7, (ab, wosb), (self.psb[bk],), signal=(kc == 7))
                        self.tt("dve", ht[:, j, f * 512:(f + 1) * 512], self.ps[bk][:, :], ht[:, j, f * 512:(f + 1) * 512],
                                ALU.add, (self.psb[bk], hb), (hb,))
            self.norm_T(ht, hb, 4, g32, g32b, yT, yTb, scr)
            if dbg is not None and i == 0:
                self.dma("sp", dbg["yT"], yT.rearrange("p c t -> p (c t)"), tuple(yTb), ())
                self.dma("sp", dbg["rr"], scr[2], tuple(scr[3]), ())
                self.dma("sp", dbg["ss"], scr[0], tuple(scr[1]), ())
            for wg in range(2):
                wv, wb = pending_w.pop(0)
                pending_w.append(load_w("w1", 1) if wg == 0 else load_w("w2", 0))
                for ml in range(16):
                    m = wg * 16 + ml
                    bk = nbank()
                    for kc in range(8):
                        self.mm(self.ps[bk][:, :], wv[:, ml // 4, kc, (ml % 4) * 128:(ml % 4 + 1) * 128], yT[:, kc, :],
                                kc == 0, kc == 7, (wb[ml // 4],) + tuple(yTb), (self.psb[bk],), signal=(kc == 7))
                    rl, rlb = relu[m % 2], relub[m % 2]
                    self.act(rl, self.ps[bk][:, :], AF.Relu, (self.psb[bk],), (rlb,))
                    self.tt("pool", hid[:, m, :], rl, rl, ALU.mult, (rlb,), (hidb[m],))
            if dbg is not None and i == 0:
                self.dma("sp", dbg["hid"], hid.rearrange("p m t -> p (m t)"), tuple(hidb), ())
            for f in range(2):
                wv, wb = pending_w.pop(0)
                if f == 0:
                    pending_w.append(load_w("w2", 1))
                elif i + 1 < ntile:
                    pending_w.append(load_w("w1", 0))
                for j in range(4):
                    bk = nbank()
                    for m in range(32):
                        self.mm(self.ps[bk][:, :], hid[:, m, j * 128:(j + 1) * 128], wv[:, m, :],
                                m == 0, m == 31, (hidb[m], wb[m // 8]), (self.psb[bk],), signal=(m == 31))
                    self.tt("dve", ht[:, j, f * 512:(f + 1) * 512], self.ps[bk][:, :], ht[:, j, f * 512:(f + 1) * 512],
                            ALU.add, (self.psb[bk], hb), (hb,))
            self.dma("sp", h_out[out_row0 + i * 512:out_row0 + (i + 1) * 512, :].rearrange("(j p) d -> p j d", p=128), ht, (hb,), ())
        self.phase_end()


    def h_tiles(self, h_ext, tiles, g1, pst=(0, 1)):
        g, gb = self.load_g32(g1)
        hts = [self.f32(4096).rearrange("p (j d) -> p j d", d=1024) for _ in range(2)]
        htb = [Buf("ht") for _ in range(2)]
        yT = self.bf(4096).rearrange("p (c t) -> p c t", t=512)
        yTb = [Buf("yT") for _ in range(4)]
        scr = self.norm_scratch(list(pst))

        def load(i):
            r0, nsub = tiles[i]
            self.dma("sp", hts[i % 2][:, 0:nsub, :], h_ext[r0:r0 + 128 * nsub, :].rearrange("(j p) d -> p j d", p=128),
                     (), (htb[i % 2],))

        load(0)
        for i in range(len(tiles)):
            if i + 1 < len(tiles):
                load(i + 1)
            self.norm_T(hts[i % 2], htb[i % 2], tiles[i][1], g, gb, yT, yTb, scr)
            yield i, yT, yTb

    def load_w_resident(self, w, w_bufs, ncols):
        wt = self.bf(8 * ncols).rearrange("p (k n) -> p k n", n=ncols)
        wb = []
        step = 512
        for c0 in range(0, ncols, step):
            b = Buf("wres")
            self.dma("sp" if (c0 // step) % 2 == 0 else "pool", wt[:, :, c0:c0 + step],
                     w[:, c0:c0 + step].rearrange("(k p) n -> p k n", p=128), tuple(w_bufs), (b,))
            wb.append(b)
        return wt, wb

    def load_gain(self, t, col, vec64, b):
        v = vec64.rearrange("(d o) -> d o", o=1)
        self.dma("sp", t[0:64, col:col + 1], v, (), (b,))
        self.dma("sp", t[64:128, col:col + 1], v, (), (b,))

    def qk_s1(self, psq, N, scr):
        sq, sqb = scr[0], scr[1]
        self.act(sq[:, 0:N], self.ps[psq][:, 0:N], AF.Square, (self.psb[psq],), (sqb,))

    def qk_s2(self, psq, N, gain, gainb, scr):
        sq, sqb, rs, rsb, qn, qnb, t1, t1b, t2, t2b, bss, brot = scr
        onesbd, onesb = self.consts["onesbd"]
        eps = self.consts["eps"][0]
        self.mm(self.ps[bss][:, 0:N], onesbd, sq[:, 0:N], True, True, (sqb, onesb), (self.psb[bss],))
        self.act(rs[:, 0:N], self.ps[bss][:, 0:N], AF.Sqrt, (self.psb[bss],), (rsb,), bias=eps[:, 0:1], scale=1.0 / 64)
        self.S.op("dve", lambda e, o=rs[:, 0:N]: e.reciprocal(o, o), (rsb,), (rsb,))
        self.stt("dve", qn[:, 0:N], self.ps[psq][:, 0:N], gain, rs[:, 0:N], ALU.mult, ALU.mult,
                 (self.psb[psq], gainb, rsb), (qnb,))

    def qk_s3(self, N, rotT, rotb, ctab, stab, tabb, scr, out_ap, outb, in_view=None):
        sq, sqb, rs, rsb, qn, qnb, t1, t1b, t2, t2b, bss, brot = scr
        self.mm(self.ps[brot][:, 0:N], rotT, qn[:, 0:N], True, True, (qnb, rotb), (self.psb[brot],))
        self.tt("pool", t1[:, 0:N], qn[:, 0:N], ctab, ALU.mult, (qnb, tabb), (t1b,))
        self.tt("dve", t2[:, 0:N], self.ps[brot][:, 0:N], stab, ALU.mult, (self.psb[brot], tabb), (t2b,))
        if in_view is None:
            self.tt("pool", out_ap, t1[:, 0:N], t2[:, 0:N], ALU.add, (t1b, t2b), (outb,))
        else:
            self.tt("pool", out_ap, in_view(t1[:, 0:N]), in_view(t2[:, 0:N]), ALU.add, (t1b, t2b), (outb,))

    def qk_pipe_push(self, pipe, job):
        pipe.append(job)
        n = len(pipe) - 1
        if job is not None:
            job["s1"]()
        if n - 1 >= 0 and pipe[n - 1] is not None:
            pipe[n - 1]["s2"]()
        if n - 2 >= 0 and pipe[n - 2] is not None:
            pipe[n - 2]["s3"]()
            if pipe[n - 2].get("done"):
                pipe[n - 2]["done"]()

    def qk_pipe_flush(self, pipe):
        self.qk_pipe_push(pipe, None)
        self.qk_pipe_push(pipe, None)

    def qk_scratch(self, bss, brot):
        sq = self.bf(512)
        rs = self.f32(512)
        qn = self.bf(512)
        t1 = self.f32(512)
        t2 = self.f32(512)
        return (sq, Buf("sq"), rs, Buf("rs"), qn, Buf("qn"), t1, Buf("t1"), t2, Buf("t2"), bss, brot)

    def phase_m1_b(self, h_ext, g1, win, win_bufs, uT, bT, ntile=8):
        tiles = [(HALO + i * 512, 4) for i in range(ntile)] + [(HALO - 128, 1), (HALO + ntile * 512, 1)]
        ucol = [128 + i * 512 for i in range(ntile)] + [0, 128 + ntile * 512]
        wt, wb = self.load_w_resident(win, win_bufs, 3072)
        tmp = [self.f32(512) for _ in range(2)]
        tmpb = [Buf("tmp") for _ in range(2)]
        ut = [self.bf(4096).rearrange("p (m t) -> p m t", t=512) for _ in range(2)]
        utb = [Buf("ut") for _ in range(2)]
        bt = [self.bf(4096).rearrange("p (m t) -> p m t", t=512) for _ in range(2)]
        btb = [Buf("bt") for _ in range(2)]
        rot = [0]

        def nbank():
            b = 2 + rot[0] % 6
            rot[0] += 1
            return b

        def proj(col0, N, yT, yTb):
            bk = nbank()
            for kc in range(8):
                self.mm(self.ps[bk][:, 0:N], wt[:, kc, col0:col0 + 128], yT[:, kc, 0:N], kc == 0, kc == 7,
                        (wb[col0 // 512],) + tuple(yTb), (self.psb[bk],), signal=(kc == 7))
            return bk

        for i, yT, yTb in self.h_tiles(h_ext, tiles, g1):
            nsub = tiles[i][1]
            N = 128 * nsub
            u, ub = ut[i % 2], utb[i % 2]
            b_, bb = bt[i % 2], btb[i % 2]
            for m in range(8):
                bc = proj(1024 + m * 128, N, yT, yTb)
                bx = proj(2048 + m * 128, N, yT, yTb)
                tp, tpb = tmp[m % 2], tmpb[m % 2]
                self.copy("act", tp[:, 0:N], self.ps[bc][:, 0:N], (self.psb[bc],), (tpb,))
                self.tt("dve", u[:, m, 0:N], self.ps[bx][:, 0:N], tp[:, 0:N], ALU.mult, (self.psb[bx], tpb), (ub,))
                if nsub == 4:
                    bg = proj(m * 128, N, yT, yTb)
                    self.copy("act", b_[:, m, :], self.ps[bg][:, :], (self.psb[bg],), (bb,))
            self.dma("sp", uT[:, ucol[i]:ucol[i] + N].rearrange("(m p) t -> p m t", p=128), u[:, :, 0:N], (ub,), ())
            if nsub == 4:
                self.dma("sp", bT[:, i * 512:(i + 1) * 512].rearrange("(m p) t -> p m t", p=128), b_, (bb,), ())
        self.phase_end()

    def phase_conv_b(self, uT, bT, conv, zT, ntile=8):
        cw = self.f32(24).rearrange("p (m k) -> p m k", k=3)
        cwb = Buf("cw")
        for k in range(3):
            self.dma("sp", cw[:, :, k:k + 1], conv[k, :].rearrange("(m p o) -> p m o", p=128, o=1), (), (cwb,), slow=True)
        uh = [self.bf(8 * 516).rearrange("p (m t) -> p m t", t=516) for _ in range(2)]
        uhb = [Buf("uh") for _ in range(2)]
        bt = [self.bf(4096).rearrange("p (m t) -> p m t", t=512) for _ in range(2)]
        btb = [Buf("bt") for _ in range(2)]
        zt = [self.bf(4096).rearrange("p (m t) -> p m t", t=512) for _ in range(2)]
        ztb = [Buf("zt") for _ in range(2)]
        acc = [self.f32(512) for _ in range(2)]
        accb = [Buf("acc") for _ in range(2)]

        def load(i):
            self.dma("sp", uh[i % 2][:, :, 0:514], uT[:, 127 + i * 512: 127 + i * 512 + 514].rearrange("(m p) t -> p m t", p=128),
                     (), (uhb[i % 2],))
            self.dma("sp", bt[i % 2], bT[:, i * 512:(i + 1) * 512].rearrange("(m p) t -> p m t", p=128), (), (btb[i % 2],))

        load(0)
        for i in range(ntile):
            if i < ntile - 1:
                load(i + 1)
            u, ub = uh[i % 2], uhb[i % 2]
            for m in range(8):
                a, ab = acc[m % 2], accb[m % 2]
                self.ts("dve", a, u[:, m, 0:512], cw[:, m, 0:1], None, ALU.mult, None, (ub, cwb), (ab,))
                self.stt("dve", a, u[:, m, 1:513], cw[:, m, 1:2], a, ALU.mult, ALU.add, (ub, cwb, ab), (ab,))
                self.stt("dve", a, u[:, m, 2:514], cw[:, m, 2:3], a, ALU.mult, ALU.add, (ub, cwb, ab), (ab,))
                self.tt("pool", zt[i % 2][:, m, :], a, bt[i % 2][:, m, :], ALU.mult, (ab, btb[i % 2]), (ztb[i % 2],))
            self.dma("sp", zT[:, i * 512:(i + 1) * 512].rearrange("(m p) t -> p m t", p=128), zt[i % 2], (ztb[i % 2],), ())
        self.phase_end()


    def phase_m1_c(self, h_ext, g1, wqkv, w_bufs, qg, kg, cosT, sinT, QT, KT, V, n_own=8):
        tiles = [(i * 512, 4) for i in range(16)]
        wt, wb = self.load_w_resident(wqkv, w_bufs, 1536)
        gains = self.f32(2)
        gb = Buf("gains")
        self.load_gain(gains, 0, qg, gb)
        self.load_gain(gains, 1, kg, gb)
        rotT, rotb = self.consts["rotC"]
        tabs = [self.f32(1024) for _ in range(3)]
        tabb = [Buf("tab") for _ in range(3)]
        qt = [self.bf(4096).rearrange("p (c t) -> p c t", t=512) for _ in range(2)]
        qtb = [Buf("qt") for _ in range(2)]
        kt = [self.bf(1024).rearrange("p (c t) -> p c t", t=512) for _ in range(2)]
        ktb = [Buf("kt") for _ in range(2)]
        vt = [self.bf(1024).rearrange("p (j n) -> p j n", n=256) for _ in range(2)]
        vtb = [Buf("vt") for _ in range(2)]
        scrs = [self.qk_scratch(4, 5), self.qk_scratch(4, 5), self.qk_scratch(4, 5)]
        rot = [0]
        ci = [0]
        pipe = []

        def load_tab(i):
            self.dma("sp", tabs[i % 3][:, 0:512], cosT[:, i * 512:(i + 1) * 512], (), (tabb[i % 3],))
            self.dma("sp", tabs[i % 3][:, 512:1024], sinT[:, i * 512:(i + 1) * 512], (), (tabb[i % 3],))

        load_tab(0)
        for i, yT, yTb in self.h_tiles(h_ext, tiles, g1, pst=(0,)):
            if i + 1 < 16:
                load_tab(i + 1)
            tab, tb = tabs[i % 3], tabb[i % 3]
            own = i < n_own
            chunks = ([("q", c) for c in range(8)] if own else []) + [("k", 0), ("k", 1)]
            for kind, c in chunks:
                col0 = c * 128 if kind == "q" else 1024 + c * 128
                bk = 1 + rot[0] % 3
                rot[0] += 1
                scr = scrs[ci[0] % 3]
                ci[0] += 1
                for kc in range(8):
                    self.mm(self.ps[bk][:, :], wt[:, kc, col0:col0 + 128], yT[:, kc, :], kc == 0, kc == 7,
                            (wb[col0 // 512],) + tuple(yTb), (self.psb[bk],), signal=(kc == 7))
                gcol = 0 if kind == "q" else 1
                o_ap, o_b = (qt[i % 2][:, c, :], qtb[i % 2]) if kind == "q" else (kt[i % 2][:, c, :], ktb[i % 2])
                job = {"s1": (lambda bk=bk, scr=scr: self.qk_s1(bk, 512, scr)),
                       "s2": (lambda bk=bk, scr=scr, gcol=gcol: self.qk_s2(bk, 512, gains[:, gcol:gcol + 1], gb, scr)),
                       "s3": (lambda scr=scr, tab=tab, tb=tb, o_ap=o_ap, o_b=o_b: self.qk_s3(
                           512, rotT, rotb, tab[:, 0:512], tab[:, 512:1024], tb, scr, o_ap, o_b))}
                if (kind, c) == chunks[-1]:
                    def done(i=i, own=own):
                        if own:
                            self.dma("sp", QT[:, i * 512:(i + 1) * 512].rearrange("(c p) t -> p c t", p=128), qt[i % 2], (qtb[i % 2],), ())
                        self.dma("sp", KT[:, i * 512:(i + 1) * 512].rearrange("(c p) t -> p c t", p=128), kt[i % 2], (ktb[i % 2],), ())
                    job["done"] = done
                self.qk_pipe_push(pipe, job)
            for j in range(4):
                bk = 6 + j % 2
                for kc in range(8):
                    self.mm(self.ps[bk][:, 0:256], yT[:, kc, j * 128:(j + 1) * 128], wt[:, kc, 1280:1536], kc == 0, kc == 7,
                            (wb[2],) + tuple(yTb), (self.psb[bk],), signal=(kc == 7))
                self.copy("act", vt[i % 2][:, j, :], self.ps[bk][:, 0:256], (self.psb[bk],), (vtb[i % 2],))
            self.dma("sp", V[i * 512:(i + 1) * 512, :].rearrange("(j p) n -> p j n", p=128), vt[i % 2], (vtb[i % 2],), ())
        self.qk_pipe_flush(pipe)
        self.phase_end()

    def phase_attn_c(self, QT, KT, V, aT, n_q=8):
        NKB = 64
        K2 = [self.bf(8192) for _ in range(2)]
        k2b = [Buf("k2") for _ in range(2)]
        VA = [self.bf(8192).rearrange("p (k n) -> p k n", n=128) for _ in range(2)]
        VB = [self.bf(8192).rearrange("p (k n) -> p k n", n=128) for _ in range(2)]
        vab = [Buf("va") for _ in range(2)]
        vbb = [Buf("vb") for _ in range(2)]
        for s_ in range(2):
            self.memset("pool", VA[s_][:, :, 64:128], 1.0, (vab[s_],))
            self.memset("pool", VB[s_][:, :, 0:64], 1.0, (vbb[s_],))
        Qs = [self.bf(512) for _ in range(2)]
        qsb = [Buf("qs") for _ in range(2)]
        NP = 8
        Pp = [self.bf(1024) for _ in range(NP // 2)]
        Ps = [Pp[i // 2][:, (i % 2) * 512:(i % 2 + 1) * 512] for i in range(NP)]
        psb_ = [Buf("P") for _ in range(NP)]
        num = self.f32(512)
        numb = Buf("num")
        denx = self.f32(512)
        denxb = Buf("denx")
        den = self.f32(512)
        denb = Buf("den")
        ao = [self.bf(512) for _ in range(2)]
        aob = [Buf("ao") for _ in range(2)]

        def load_kv(j):
            s_ = j % 2
            self.dma("sp", K2[s_][0:64, :], KT[j * 64:(j + 1) * 64, :], (), (k2b[s_],))
            self.dma("pool", K2[s_][64:128, :], KT[j * 64:(j + 1) * 64, :], (), (k2b[s_],))
            vsrc = V[:, j * 64:(j + 1) * 64].rearrange("(k p) d -> p k d", p=128)
            self.dma("sp", VA[s_][:, :, 0:64], vsrc, (), (vab[s_],))
            self.dma("pool", VB[s_][:, :, 64:128], vsrc, (), (vbb[s_],))

        units = [(j, hp, qc) for j in range(4) for hp in (2 * j, 2 * j + 1) for qc in range(n_q)]
        steps = [(u, hh, kb) for u in range(len(units)) for kb in range(NKB) for hh in range(2)]
        LA = 4

        def load_q(u):
            j, hp, qc = units[u]
            self.dma("sp", Qs[u % 2], QT[hp * 128:(hp + 1) * 128, qc * 512:(qc + 1) * 512], (), (qsb[u % 2],))

        load_kv(0)
        load_q(0)
        def emit_qk(idx):
            u, hh, kb = steps[idx]
            j, hp, qc = units[u]
            s_ = j % 2
            if hh == 0 and kb == 0:
                if u + 1 < len(units):
                    load_q(u + 1)
            sb = idx % 4
            pr = slice(hh * 64, (hh + 1) * 64)
            self.mm(self.ps[sb][:, :], K2[s_][pr, kb * 128:(kb + 1) * 128], Qs[u % 2][pr, :], True, True,
                    (k2b[s_], qsb[u % 2]), (self.psb[sb],))

        def emit_exp(idx):
            sb = idx % 4
            self.act(Ps[idx % NP], self.ps[sb][:, :], AF.Exp, (self.psb[sb],), (psb_[idx % NP],), scale=0.125)

        def emit_pv(i2):
            u, hh, kb = steps[i2]
            j, hp, qc = units[u]
            s_ = j % 2
            if hh == 0 and kb == 0 and hp == 2 * j and qc == 0 and j + 1 < 4:
                load_kv(j + 1)
            po = 4 + (u % 2) * 2 + hh
            vv, vvb = (VA[s_], vab[s_]) if hh == 0 else (VB[s_], vbb[s_])
            self.mm(self.ps[po][:, :], vv[:, kb, :], Ps[i2 % NP], kb == 0, kb == NKB - 1,
                    (vvb, psb_[i2 % NP]), (self.psb[po],), signal=(kb == NKB - 1))
            if hh == 1 and kb == NKB - 1:
                pa, pb = 4 + (u % 2) * 2, 4 + (u % 2) * 2 + 1
                self.copy("dve", num[0:64, :], self.ps[pa][0:64, :], (self.psb[pa],), (numb,))
                self.copy("dve", denx[64:128, :], self.ps[pa][64:128, :], (self.psb[pa],), (denxb,))
                self.copy("dve", num[64:128, :], self.ps[pb][64:128, :], (self.psb[pb],), (numb,))
                self.copy("dve", denx[0:64, :], self.ps[pb][0:64, :], (self.psb[pb],), (denxb,))
                self.dma("sp", den[0:64, :], denx[64:128, :], (denxb,), (denb,))
                self.dma("sp", den[64:128, :], denx[0:64, :], (denxb,), (denb,))
                self.S.op("dve", lambda e, o=den: e.reciprocal(o, o), (denb,), (denb,))
                self.tt("dve", ao[u % 2], num, den, ALU.mult, (numb, denb), (aob[u % 2],))
                self.dma("sp", aT[hp * 128:(hp + 1) * 128, qc * 512:(qc + 1) * 512], ao[u % 2], (aob[u % 2],), ())

        for base in range(0, len(steps) + LA, 2):
            for idx in (base, base + 1):
                if idx < len(steps):
                    emit_qk(idx)
            if base + 1 < len(steps):
                sb = base % 4
                self.act(Pp[(base % NP) // 2], self.ps2[sb // 2][:, :], AF.Exp, (self.psb[sb], self.psb[sb + 1]),
                         (psb_[base % NP], psb_[(base + 1) % NP]), scale=0.125)
            for idx in (base, base + 1):
                if 0 <= idx - LA < len(steps):
                    emit_pv(idx - LA)
        self.phase_end()


    def phase_m1_a(self, h_ext, g1, wqkv, w_bufs, qg, kg, cosT, sinT, valid, QTg, KTg, Vg, TOKn=TOK):
        NT = (TOKn + 2 * HALO) // 512
        tiles = [(i * 512, 4) for i in range(NT)]
        items = []
        for it in range(NT):
            for g in range(3):
                Dg = A_DIL[g]
                halo = 64 * Dg
                t0 = it * 512 - HALO
                lo, hi = max(t0, -halo), min(t0 + 512, TOKn + halo)
                if lo >= hi:
                    continue
                own = 0 <= t0 < TOKn
                for s_ in range(3):
                    if s_ == 0 and not own:
                        continue
                    items.append((it, g, s_, lo - t0, hi - t0))
        gains = self.f32(6)
        gb = Buf("gains")
        for g in range(3):
            self.load_gain(gains, g, qg[g, :], gb)
            self.load_gain(gains, 3 + g, kg[g, :], gb)
        vld = self.f32(4 * NT)
        vldb = Buf("valid")
        self.dma("sp", vld, valid, (), (vldb,))
        ones3 = self.bf(1024).rearrange("p (h n) -> p h n", n=128)
        ones3b = Buf("ones3")
        self.memset("pool", ones3, 1.0, (ones3b,))
        rotT, rotb = self.consts["rotA"]
        tabs = [self.f32(1024) for _ in range(3)]
        tabb = [Buf("tab") for _ in range(3)]
        NS = 3
        wsl = [self.bf(8192).rearrange("p (k n) -> p k n", n=1024) for _ in range(NS)]
        wslb = [[Buf("ws") for _ in range(2)] for _ in range(NS)]
        ot = [self.bf(4096).rearrange("p (c t) -> p c t", t=512) for _ in range(2)]
        otb = [Buf("ot") for _ in range(2)]
        vt = [self.bf(8192).rearrange("p (j h n) -> p j h n", j=4, h=8) for _ in range(2)]
        vtb = [Buf("vt") for _ in range(2)]
        scrs = [self.qk_scratch(4, 5), self.qk_scratch(4, 5), self.qk_scratch(4, 5)]
        ci = [0]
        rotv = [0]
        pipe = []

        def load_w(n):
            it, g, s_, c0, c1 = items[n]
            cb = g * 3 + s_
            for hf in range(2):
                self.dma("sp" if hf == 0 else "pool", wsl[n % NS][:, :, hf * 512:(hf + 1) * 512],
                         wqkv[:, cb * 1024 + hf * 512: cb * 1024 + (hf + 1) * 512].rearrange("(k p) n -> p k n", p=128),
                         tuple(w_bufs), (wslb[n % NS][hf],))

        def load_tab(i):
            self.dma("sp", tabs[i % 3][:, 0:512], cosT[:, i * 512:(i + 1) * 512], (), (tabb[i % 3],))
            self.dma("sp", tabs[i % 3][:, 512:1024], sinT[:, i * 512:(i + 1) * 512], (), (tabb[i % 3],))

        load_w(0)
        load_w(1)
        load_tab(0)
        gen = self.h_tiles(h_ext, tiles, g1, pst=(0,))
        cur = -1
        yT = yTb = None
        rot = [0]
        oi = [0]
        vi = [0]
        for n, (it, g, s_, c0, c1) in enumerate(items):
            if it != cur:
                cur, yT, yTb = next(gen)
                assert cur == it
                if it + 1 < NT:
                    load_tab(it + 1)
            if n + 2 < len(items):
                load_w(n + 2)
            tab, tb = tabs[it % 3], tabb[it % 3]
            ws, wb = wsl[n % NS], wslb[n % NS]
            Dg = A_DIL[g]
            N = c1 - c0
            t0 = it * 512 - HALO
            if s_ < 2:
                o, ob = ot[oi[0] % 2], otb[oi[0] % 2]
                oi[0] += 1
                nm = N // Dg
                if s_ == 0:
                    m0 = (t0 + c0) // Dg
                    dst = QTg[g]
                else:
                    m0 = (t0 + c0) // Dg + 64
                    dst = KTg[g]
                for c in range(8):
                    bk = 1 + rot[0] % 3
                    rot[0] += 1
                    scr = scrs[ci[0] % 3]
                    ci[0] += 1
                    for kc in range(8):
                        self.mm(self.ps[bk][:, 0:N], ws[:, kc, c * 128:(c + 1) * 128], yT[:, kc, c0:c1], kc == 0, kc == 7,
                                (wb[c // 4],) + tuple(yTb), (self.psb[bk],), signal=(kc == 7))
                    gcol = g if s_ == 0 else 3 + g
                    out_ap = o[:, c, 0:N].rearrange("p (r m) -> p m r", r=Dg)
                    iv = (lambda ap, Dg=Dg: ap.rearrange("p (m r) -> p m r", r=Dg))
                    job = {"s1": (lambda bk=bk, scr=scr, N=N: self.qk_s1(bk, N, scr)),
                           "s2": (lambda bk=bk, scr=scr, N=N, gcol=gcol: self.qk_s2(bk, N, gains[:, gcol:gcol + 1], gb, scr)),
                           "s3": (lambda scr=scr, N=N, tab=tab, tb=tb, c0=c0, c1=c1, out_ap=out_ap, ob=ob, iv=iv: self.qk_s3(
                               N, rotT, rotb, tab[:, c0:c1], tab[:, 512 + c0:512 + c1], tb, scr, out_ap, ob, in_view=iv))}
                    if c == 7:
                        def done(o=o, ob=ob, dst=dst, m0=m0, nm=nm, N=N, Dg=Dg):
                            for cc in range(8):
                                self.dma("sp" if cc % 2 == 0 else "pool", dst[cc * 128:(cc + 1) * 128, :, m0:m0 + nm],
                                         o[:, cc, 0:N].rearrange("p (r m) -> p r m", r=Dg), (ob,), ())
                        job["done"] = done
                    self.qk_pipe_push(pipe, job)
            else:
                v, vb = vt[vi[0] % 2], vtb[vi[0] % 2]
                vi[0] += 1
                js = [j for j in range(4) if j * 128 < c1 and (j + 1) * 128 > c0]
                for j in js:
                    for hf in range(2):
                        bk = 6 + rotv[0] % 2
                        rotv[0] += 1
                        for kc in range(8):
                            self.mm(self.ps[bk][:, :], yT[:, kc, j * 128:(j + 1) * 128], ws[:, kc, hf * 512:(hf + 1) * 512],
                                    kc == 0, kc == 7, (wb[hf],) + tuple(yTb), (self.psb[bk],), signal=(kc == 7))
                        outv = v.rearrange("p j h (x y) -> p j h x y", y=64)[:, j, 4 * hf:4 * hf + 4, 0:4:3, :]
                        self.copy("act", outv, self.ps[bk][:, :].rearrange("p (q a d) -> p q a d", a=2, d=64),
                                  (self.psb[bk],), (vb,))
                    self.ts("dve", v[:, j, :, 64:192], ones3, vld[:, it * 4 + j:it * 4 + j + 1], None, ALU.mult, None,
                            (ones3b, vldb), (vb,))
                    r0 = it * 512 + j * 128
                    self.dma("sp" if j % 2 == 0 else "pool", Vg[g][r0:r0 + 128, :, :], v[:, j, :, :], (vb,), ())
        self.qk_pipe_flush(pipe)
        self.phase_end()

    def phase_attn_a(self, QTg, KTg, Vg, aT, dbg=None, seg_off=0):
        mask, maskb = self.consts["maskA"]
        AccA = self.f32(4096)
        AccB = self.f32(4096)
        accb = [Buf("accA"), Buf("accB")]
        Acc = [AccA, AccB]
        tmpD = self.f32(4096)
        tmpDb = Buf("tmpD")
        ao = self.bf(4096)
        aob = Buf("ao")
        NSL = 3
        KTs = [self.bf(4224) for _ in range(NSL)]
        QTs = [self.bf(4096) for _ in range(NSL)]
        Vs = [self.bf(33 * 256).rearrange("p (i n) -> p i n", n=256) for _ in range(NSL)]
        slb = [[Buf("kts"), Buf("qts"), Buf("vs")] for _ in range(NSL)]
        NP = 4
        Ps = [self.bf(512) for _ in range(NP)]
        Pb = [Buf("P") for _ in range(NP)]
        Pm = [self.bf(512) for _ in range(NP)]
        Pmb = [Buf("Pm") for _ in range(NP)]
        units = [(hp, g, r) for hp in range(8) for g in range(3) for r in range(A_DIL[g])]

        def load_unit(u):
            hp, g, r = units[u]
            Dg = A_DIL[g]
            Lc = TOK // Dg
            nkb = Lc // 128 + 1
            s_ = u % NSL
            so = seg_off // Dg
            self.dma("sp", KTs[s_][:, 0:Lc + 128], KTg[g][hp * 128:(hp + 1) * 128, r, so:so + Lc + 128], (), (slb[s_][0],))
            self.dma("pool", QTs[s_][:, 0:Lc], QTg[g][hp * 128:(hp + 1) * 128, r, so:so + Lc], (), (slb[s_][1],))
            R0 = HALO + seg_off - 64 * Dg + r
            n = 128 * nkb
            src = Vg[g][R0:R0 + (n - 1) * Dg + 1:Dg, hp, :].rearrange("(i k) n -> k i n", k=128)
            self.dma("sp", Vs[s_][:, 0:nkb, :], src, (), (slb[s_][2],))

        steps = []
        for u, (hp, g, r) in enumerate(units):
            Lc = TOK // A_DIL[g]
            for qc in range(Lc // 256):
                for hh in range(2):
                    steps.append((u, qc, hh))
        LA = 2
        load_unit(0)
        first_of_unit = {}
        last_of_hp = {}
        for i, (u, qc, hh) in enumerate(steps):
            first_of_unit.setdefault(u, i)
            last_of_hp[units[u][0]] = i
        def emit_qk(idx):
            u, qc, hh = steps[idx]
            hp, g, r = units[u]
            s_ = u % NSL
            if first_of_unit[u] == idx and u + 1 < len(units):
                load_unit(u + 1)
            Q0 = qc * 256
            pr = slice(hh * 64, (hh + 1) * 64)
            sb = idx % 4
            kt, qt = KTs[s_], QTs[s_]
            rd = (slb[s_][0], slb[s_][1])
            self.mm(self.ps[sb][:, 0:128], kt[pr, Q0:Q0 + 128], qt[pr, Q0:Q0 + 128], True, True, rd, (self.psb[sb],), signal=False)
            self.mm(self.ps[sb][:, 128:384], kt[pr, Q0 + 128:Q0 + 256], qt[pr, Q0:Q0 + 256], True, True, rd, (self.psb[sb],), signal=False)
            self.mm(self.ps[sb][:, 384:512], kt[pr, Q0 + 256:Q0 + 384], qt[pr, Q0 + 128:Q0 + 256], True, True, rd, (self.psb[sb],))

        def emit_exp(idx):
            sb = idx % 4
            self.act(Ps[idx % NP], self.ps[sb][:, :], AF.Exp, (self.psb[sb],), (Pb[idx % NP],), scale=0.125)
            self.tt("dve", Pm[idx % NP], Ps[idx % NP], mask, ALU.mult, (Pb[idx % NP], maskb), (Pmb[idx % NP],))

        def emit_pv(i2):
            u, qc, hh = steps[i2]
            hp, g, r = units[u]
            Dg = A_DIL[g]
            s_ = u % NSL
            ob = 4 + i2 % 4
            vs = Vs[s_]
            hs = slice(hh * 128, (hh + 1) * 128)
            pm = Pm[i2 % NP]
            rd = (slb[s_][2], Pmb[i2 % NP])
            self.mm(self.ps[ob][:, 0:256], vs[:, 2 * qc + 1, hs], pm[:, 128:384], True, False, rd, (self.psb[ob],), signal=False)
            self.mm(self.ps[ob][:, 0:128], vs[:, 2 * qc, hs], pm[:, 0:128], False, False, rd, (self.psb[ob],), signal=False)
            self.mm(self.ps[ob][:, 128:256], vs[:, 2 * qc + 2, hs], pm[:, 384:512], False, True, rd, (self.psb[ob],))
            Q0 = qc * 256
            accv = Acc[hh].rearrange("p (m r) -> p r m", r=Dg)[:, r, Q0:Q0 + 256]
            if g == 0:
                self.copy("dve", accv, self.ps[ob][:, 0:256], (self.psb[ob],), (accb[hh],))
            else:
                self.tt("dve", accv, self.ps[ob][:, 0:256], accv, ALU.add, (self.psb[ob], accb[hh]), (accb[hh],))
            if last_of_hp[hp] == i2:
                if dbg is not None and hp == 0:
                    self.dma("sp", dbg[0], AccA, (accb[0],), ())
                    self.dma("sp", dbg[1], AccB, (accb[1],), ())
                self.dma("sp", tmpD[0:64, :], AccA[64:128, :], (accb[0],), (tmpDb,))
                self.dma("sp", tmpD[64:128, :], AccB[0:64, :], (accb[1],), (tmpDb,))
                self.S.op("dve", lambda e, o=tmpD: e.reciprocal(o, o), (tmpDb,), (tmpDb,))
                self.tt("dve", ao[0:64, :], AccA[0:64, :], tmpD[0:64, :], ALU.mult, (accb[0], tmpDb), (aob,))
                self.tt("dve", ao[64:128, :], AccB[64:128, :], tmpD[64:128, :], ALU.mult, (accb[1], tmpDb), (aob,))
                self.dma("sp", aT[hp * 128:(hp + 1) * 128, seg_off:seg_off + TOK], ao, (aob,), ())

        for base in range(0, len(steps) + LA, 2):
            for idx in (base, base + 1):
                if idx < len(steps):
                    emit_qk(idx)
            for idx in (base, base + 1):
                if idx < len(steps):
                    emit_exp(idx)
            for idx in (base, base + 1):
                if 0 <= idx - LA < len(steps):
                    emit_pv(idx - LA)
        self.phase_end()


    def phase_blend(self, flags, jobs):
        fl = self.f32(2)
        flb = Buf("flags")
        self.dma("sp", fl, flags, (), (flb,))
        ta = [self.f32(4096).rearrange("p (j d) -> p j d", d=1024) for _ in range(2)]
        tb = [self.f32(4096).rearrange("p (j d) -> p j d", d=1024) for _ in range(2)]
        tab_ = [Buf("ta") for _ in range(2)]
        tbb_ = [Buf("tb") for _ in range(2)]
        k = 0
        for dst, A, fa, B, fb in jobs:
            n = dst.shape[0]
            for c in range(n // 512):
                rs = slice(c * 512, (c + 1) * 512)
                a, ab = ta[k % 2], tab_[k % 2]
                b, bb = tb[k % 2], tbb_[k % 2]
                k += 1
                src0, f0 = (A, fa) if A is not None else (B, fb)
                self.dma("sp", a, src0[rs, :].rearrange("(j p) d -> p j d", p=128), (), (ab,))
                self.ts("dve", a, a, fl[:, f0:f0 + 1], None, ALU.mult, None, (ab, flb), (ab,))
                if A is not None and B is not None:
                    self.dma("pool", b, B[rs, :].rearrange("(j p) d -> p j d", p=128), (), (bb,))
                    self.stt("dve", a, b, fl[:, fb:fb + 1], a, ALU.mult, ALU.add, (bb, flb, ab), (ab,))
                self.dma("sp", dst[rs, :].rearrange("(j p) d -> p j d", p=128), a, (ab,), ())
        self.phase_end()

    def zero_rows(self, dsts):
        z = self.f32(4096).rearrange("p (j d) -> p j d", d=1024)
        zb = Buf("z")
        self.memset("dve", z, 0.0, (zb,))
        for dst in dsts:
            n = dst.shape[0]
            for c in range(n // 512):
                self.dma("sp", dst[c * 512:(c + 1) * 512, :].rearrange("(j p) d -> p j d", p=128), z, (zb,), ())
        self.phase_end()

def _consts_common():
    ident = np.eye(128, dtype=np.float32).astype(NPBF)
    return {"ident": ident}


def build_ffn_only():
    kb = KB()
    nc = kb.nc
    h_in = nc.dram_tensor("h_in", [TOK, DM], F32, kind="ExternalInput").ap()
    g2 = nc.dram_tensor("g2", [DM], F32, kind="ExternalInput").ap()
    w1 = nc.dram_tensor("w1", [DM, DFF], F32, kind="ExternalInput").ap()
    w2 = nc.dram_tensor("w2", [DFF, DM], F32, kind="ExternalInput").ap()
    ident = nc.dram_tensor("ident", [128, 128], BF16, kind="ExternalInput").ap()
    h_out = nc.dram_tensor("h_out", [TOK, DM], F32, kind="ExternalOutput").ap()
    kb.load_consts({"ident": (ident, BF16, 128)})
    w1b, w1bufs = kb.cvt(w1)
    w2b, w2bufs = kb.cvt(w2)
    dbg = {"yT": nc.dram_tensor("dbg_yT", [128, 4096], BF16, kind="ExternalOutput").ap(),
           "hid": nc.dram_tensor("dbg_hid", [128, 32 * 512], BF16, kind="ExternalOutput").ap(),
           "rr": nc.dram_tensor("dbg_rr", [128, 4], F32, kind="ExternalOutput").ap(),
           "ss": nc.dram_tensor("dbg_ss", [128, 4], F32, kind="ExternalOutput").ap()}
    kb.phase_out_ffn(h_in, 0, None, None, None, g2, w1b, w1bufs, w2b, w2bufs, h_out, dbg=dbg)
    with nc.Block() as block:
        kb.S.emit(block)
    kb.es.close()
    return nc


def _declare(nc, name, shape, dt, kind="ExternalInput"):
    return nc.dram_tensor(name, list(shape), dt, kind=kind).ap()


def build_layer_b():
    kb = KB()
    nc = kb.nc
    h_ext = _declare(nc, "h_ext", [EXT, DM], F32)
    g1 = _declare(nc, "g1", [DM], F32)
    g2 = _declare(nc, "g2", [DM], F32)
    wa = _declare(nc, "wa", [DM, 3072], F32)
    wo = _declare(nc, "wo", [DM, DM], F32)
    conv = _declare(nc, "conv", [3, DM], F32)
    w1 = _declare(nc, "w1", [DM, DFF], F32)
    w2 = _declare(nc, "w2", [DFF, DM], F32)
    ident = _declare(nc, "ident", [128, 128], BF16)
    h_out = _declare(nc, "h_out", [TOK, DM], F32, kind="ExternalOutput")
    kb.load_consts({"ident": (ident, BF16, 128)})
    wab, wabufs = kb.cvt(wa)
    wob, wobufs = kb.cvt(wo)
    w1b, w1bufs = kb.cvt(w1)
    w2b, w2bufs = kb.cvt(w2)
    uT = kb.dram([DM, TOK + 256], BF16)
    bT = kb.dram([DM, TOK], BF16)
    zT = kb.dram([DM, TOK], BF16)
    kb.phase_m1_b(h_ext, g1, wab, wabufs, uT, bT)
    kb.phase_conv_b(uT, bT, conv, zT)
    kb.phase_out_ffn(h_ext, HALO, zT, wob, wobufs, g2, w1b, w1bufs, w2b, w2bufs, h_out)
    with nc.Block() as block:
        kb.S.emit(block)
    kb.es.close()
    return nc


def rope_tables_c(pos):
    inv = (10000.0 ** (-np.arange(0, 32, 2, dtype=np.float32) / np.float32(32))).astype(np.float32)
    row = (pos // 64).astype(np.float32)
    col = (pos % 64).astype(np.float32)
    ang_r = row[None, :] * inv[:, None]
    ang_c = col[None, :] * inv[:, None]
    cos64 = np.concatenate([np.cos(ang_r), np.cos(ang_r), np.cos(ang_c), np.cos(ang_c)], 0)
    sin64 = np.concatenate([np.sin(ang_r), np.sin(ang_r), np.sin(ang_c), np.sin(ang_c)], 0)
    return (np.concatenate([cos64, cos64], 0).astype(np.float32), np.concatenate([sin64, sin64], 0).astype(np.float32))


def rot_matrix(kind):
    R = np.zeros((128, 128), np.float32)
    for b in (0, 64):
        if kind == "C":
            for off in (0, 32):
                for i in range(16):
                    R[b + off + i + 16, b + off + i] = -1.0
                    R[b + off + i, b + off + i + 16] = 1.0
        else:
            for i in range(8):
                R[b + i + 8, b + i] = -1.0
                R[b + i, b + i + 8] = 1.0
    return R.astype(NPBF)


def onesbd_matrix():
    M = np.zeros((128, 128), np.float32)
    M[0:64, 0:64] = 1.0
    M[64:128, 64:128] = 1.0
    return M.astype(NPBF)


def build_layer_c():
    kb = KB()
    nc = kb.nc
    h_ext = _declare(nc, "h_ext", [SEQ, DM], F32)
    g1 = _declare(nc, "g1", [DM], F32)
    g2 = _declare(nc, "g2", [DM], F32)
    wa = _declare(nc, "wa", [DM, 1536], F32)
    wo = _declare(nc, "wo", [DM, DM], F32)
    qg = _declare(nc, "qg", [64], F32)
    kg = _declare(nc, "kg", [64], F32)
    w1 = _declare(nc, "w1", [DM, DFF], F32)
    w2 = _declare(nc, "w2", [DFF, DM], F32)
    ident = _declare(nc, "ident", [128, 128], BF16)
    onesbd = _declare(nc, "onesbd", [128, 128], BF16)
    rotC = _declare(nc, "rotC", [128, 128], BF16)
    cosT = _declare(nc, "cosT", [128, SEQ], F32)
    sinT = _declare(nc, "sinT", [128, SEQ], F32)
    h_out = _declare(nc, "h_out", [TOK, DM], F32, kind="ExternalOutput")
    kb.load_consts({"ident": (ident, BF16, 128), "onesbd": (onesbd, BF16, 128), "rotC": (rotC, BF16, 128)})
    wab, wabufs = kb.cvt(wa)
    wob, wobufs = kb.cvt(wo)
    w1b, w1bufs = kb.cvt(w1)
    w2b, w2bufs = kb.cvt(w2)
    QT = kb.dram([DM, TOK], BF16)
    KT = kb.dram([256, SEQ], BF16)
    V = kb.dram([SEQ, 256], BF16)
    aT = kb.dram([DM, TOK], BF16)
    kb.phase_m1_c(h_ext, g1, wab, wabufs, qg, kg, cosT, sinT, QT, KT, V)
    kb.phase_attn_c(QT, KT, V, aT)
    kb.phase_out_ffn(h_ext, 0, aT, wob, wobufs, g2, w1b, w1bufs, w2b, w2bufs, h_out)
    with nc.Block() as block:
        kb.S.emit(block)
    kb.es.close()
    return nc


def rope_tables_a(pos):
    inv = (500000.0 ** (-np.arange(0, 16, 2, dtype=np.float32) / np.float32(16))).astype(np.float32)
    ang = pos.astype(np.float32)[None, :] * inv[:, None]
    n = pos.shape[0]
    cos64 = np.ones((64, n), np.float32)
    sin64 = np.zeros((64, n), np.float32)
    cos64[0:8] = np.cos(ang)
    cos64[8:16] = np.cos(ang)
    sin64[0:8] = np.sin(ang)
    sin64[8:16] = np.sin(ang)
    return np.concatenate([cos64, cos64], 0), np.concatenate([sin64, sin64], 0)


def band_mask():
    k = np.arange(128)[:, None]
    m = np.zeros((128, 512), np.float32)
    j = np.arange(128)[None, :]
    m[:, 0:128] = (j <= k)
    j2 = np.arange(256)[None, :]
    m[:, 128:384] = (j2 >= k) & (j2 <= k + 128)
    m[:, 384:512] = (j >= k)
    return m.astype(NPBF)


def build_layer_a(stop=3, debug=False):
    kb = KB()
    dk = 'ExternalOutput' if debug else 'Internal'
    nc = kb.nc
    h_ext = _declare(nc, "h_ext", [EXT, DM], F32)
    g1 = _declare(nc, "g1", [DM], F32)
    g2 = _declare(nc, "g2", [DM], F32)
    wa = _declare(nc, "wa", [DM, 9216], F32)
    wo = _declare(nc, "wo", [DM, DM], F32)
    qg = _declare(nc, "qg", [3, 64], F32)
    kg = _declare(nc, "kg", [3, 64], F32)
    w1 = _declare(nc, "w1", [DM, DFF], F32)
    w2 = _declare(nc, "w2", [DFF, DM], F32)
    ident = _declare(nc, "ident", [128, 128], BF16)
    onesbd = _declare(nc, "onesbd", [128, 128], BF16)
    rotA = _declare(nc, "rotA", [128, 128], BF16)
    maskA = _declare(nc, "maskA", [128, 512], BF16)
    cosT = _declare(nc, "cosT", [128, EXT], F32)
    sinT = _declare(nc, "sinT", [128, EXT], F32)
    valid = _declare(nc, "valid", [128, EXT // 128], F32)
    h_out = _declare(nc, "h_out", [TOK, DM], F32, kind="ExternalOutput")
    kb.load_consts({"ident": (ident, BF16, 128), "onesbd": (onesbd, BF16, 128), "rotA": (rotA, BF16, 128),
                    "maskA": (maskA, BF16, 512)})
    wab, wabufs = kb.cvt(wa)
    wob, wobufs = kb.cvt(wo)
    w1b, w1bufs = kb.cvt(w1)
    w2b, w2bufs = kb.cvt(w2)
    QTg = [kb.dram([DM, A_DIL[g], TOK // A_DIL[g]], BF16, name="dbg_qt%d" % g, kind=dk) for g in range(3)]
    KTg = [kb.dram([DM, A_DIL[g], TOK // A_DIL[g] + 128], BF16, name="dbg_kt%d" % g, kind=dk) for g in range(3)]
    Vg = [kb.dram([EXT, 8, 256], BF16, name="dbg_v%d" % g, kind=dk) for g in range(3)]
    aT = kb.dram([DM, TOK], BF16, name="dbg_aT", kind=dk)
    kb.phase_m1_a(h_ext, g1, wab, wabufs, qg, kg, cosT, sinT, valid, QTg, KTg, Vg)
    if stop >= 2:
        dbg = None
        if debug:
            dbg = [kb.dram([128, TOK], F32, name="dbg_accA", kind=dk), kb.dram([128, TOK], F32, name="dbg_accB", kind=dk)]
        kb.phase_attn_a(QTg, KTg, Vg, aT, dbg)
    if stop >= 3:
        kb.phase_out_ffn(h_ext, HALO, aT if stop >= 2 else None, wob, wobufs, g2, w1b, w1bufs, w2b, w2bufs, h_out)
    with nc.Block() as block:
        kb.S.emit(block)
    kb.es.close()
    return nc


SEQX = SEQ + 2 * HALO
NQ2 = 5120


def build_fused():
    kb = KB()
    nc = kb.nc
    D_ = _declare
    x_ext = D_(nc, "x_ext", [SEQX, DM], F32)
    g1 = D_(nc, "g1", [4, DM], F32)
    g2 = D_(nc, "g2", [4, DM], F32)
    a_wqkv = [D_(nc, "a_wqkv%d" % i, [DM, 9216], F32) for i in range(2)]
    a_wo = [D_(nc, "a_wo%d" % i, [DM, DM], F32) for i in range(2)]
    a_qg = [D_(nc, "a_qg%d" % i, [3, 64], F32) for i in range(2)]
    a_kg = [D_(nc, "a_kg%d" % i, [3, 64], F32) for i in range(2)]
    b_win = D_(nc, "b_win", [DM, 3072], F32)
    b_wout = D_(nc, "b_wout", [DM, DM], F32)
    b_conv = D_(nc, "b_conv", [3, DM], F32)
    c_wqkv = D_(nc, "c_wqkv", [DM, 1536], F32)
    c_wo = D_(nc, "c_wo", [DM, DM], F32)
    c_qg = D_(nc, "c_qg", [64], F32)
    c_kg = D_(nc, "c_kg", [64], F32)
    w1 = [D_(nc, "w1_%d" % i, [DM, DFF], F32) for i in range(4)]
    w2 = [D_(nc, "w2_%d" % i, [DFF, DM], F32) for i in range(4)]
    ident = D_(nc, "ident", [128, 128], BF16)
    onesbd = D_(nc, "onesbd", [128, 128], BF16)
    rotA = D_(nc, "rotA", [128, 128], BF16)
    rotC = D_(nc, "rotC", [128, 128], BF16)
    maskA = D_(nc, "maskA", [128, 512], BF16)
    cosA0 = D_(nc, "cosA0", [128, SEQX], F32)
    sinA0 = D_(nc, "sinA0", [128, SEQX], F32)
    validA0 = D_(nc, "validA0", [128, SEQX // 128], F32)
    cosA3 = D_(nc, "cosA3", [128, EXT], F32)
    sinA3 = D_(nc, "sinA3", [128, EXT], F32)
    validA3 = D_(nc, "validA3", [128, EXT // 128], F32)
    cosC = D_(nc, "cosC", [128, SEQ], F32)
    sinC = D_(nc, "sinC", [128, SEQ], F32)
    flags = D_(nc, "flags", [128, 2], F32)
    h_out = D_(nc, "h_out", [TOK, DM], F32, kind="ExternalOutput")

    kb.load_consts({"ident": (ident, BF16, 128), "onesbd": (onesbd, BF16, 128), "rotA": (rotA, BF16, 128),
                    "rotC": (rotC, BF16, 128), "maskA": (maskA, BF16, 512)})

    wq0 = kb.cvt(a_wqkv[0])
    wo0 = kb.cvt(a_wo[0])
    f1_0 = kb.cvt(w1[0])
    f2_0 = kb.cvt(w2[0])
    h1_ext = kb.dram([SEQX, DM], F32)
    kb.zero_rows([h1_ext[0:HALO, :], h1_ext[HALO + SEQ:SEQX, :]])
    QTg0 = [kb.dram([DM, A_DIL[g], SEQ // A_DIL[g]], BF16) for g in range(3)]
    KTg0 = [kb.dram([DM, A_DIL[g], SEQ // A_DIL[g] + 128], BF16) for g in range(3)]
    Vg0 = [kb.dram([SEQX, 8, 256], BF16) for g in range(3)]
    aT0 = kb.dram([DM, SEQ], BF16)
    kb.phase_m1_a(x_ext, g1[0, :], wq0[0], wq0[1], a_qg[0], a_kg[0], cosA0, sinA0, validA0, QTg0, KTg0, Vg0, TOKn=SEQ)
    bw = kb.cvt(b_win)
    bo = kb.cvt(b_wout)
    f1_1 = kb.cvt(w1[1])
    f2_1 = kb.cvt(w2[1])
    for seg in range(2):
        kb.phase_attn_a(QTg0, KTg0, Vg0, aT0, None, seg_off=seg * TOK)
    kb.phase_out_ffn(x_ext, HALO, aT0, wo0[0], wo0[1], g2[0, :], f1_0[0], f1_0[1], f2_0[0], f2_0[1], h1_ext,
                     ntile=16, out_row0=HALO)

    uT = kb.dram([DM, SEQ + 256], BF16)
    bT = kb.dram([DM, SEQ], BF16)
    zT = kb.dram([DM, SEQ], BF16)
    h2 = kb.dram([SEQ, DM], F32)
    kb.phase_m1_b(h1_ext, g1[1, :], bw[0], bw[1], uT, bT, ntile=16)
    cw = kb.cvt(c_wqkv)
    co = kb.cvt(c_wo)
    f1_2 = kb.cvt(w1[2])
    f2_2 = kb.cvt(w2[2])
    kb.phase_conv_b(uT, bT, b_conv, zT, ntile=16)
    kb.phase_out_ffn(h1_ext, HALO, zT, bo[0], bo[1], g2[1, :], f1_1[0], f1_1[1], f2_1[0], f2_1[1], h2, ntile=16)

    h2c = kb.dram([SEQ, DM], F32)
    kb.phase_blend(flags, [(h2c[0:NQ2, :], h2[0:NQ2, :], 0, h2[SEQ - NQ2:SEQ, :], 1),
                           (h2c[NQ2:SEQ, :], h2[NQ2:SEQ, :], 0, h2[0:SEQ - NQ2, :], 1)])
    QT = kb.dram([DM, NQ2], BF16)
    KT = kb.dram([256, SEQ], BF16)
    V = kb.dram([SEQ, 256], BF16)
    aT2 = kb.dram([DM, NQ2], BF16)
    h3c = kb.dram([NQ2, DM], F32)
    kb.phase_m1_c(h2c, g1[2, :], cw[0], cw[1], c_qg, c_kg, cosC, sinC, QT, KT, V, n_own=NQ2 // 512)
    wq3 = kb.cvt(a_wqkv[1])
    wo3 = kb.cvt(a_wo[1])
    f1_3 = kb.cvt(w1[3])
    f2_3 = kb.cvt(w2[3])
    kb.phase_attn_c(QT, KT, V, aT2, n_q=NQ2 // 512)
    kb.phase_out_ffn(h2c, 0, aT2, co[0], co[1], g2[2, :], f1_2[0], f1_2[1], f2_2[0], f2_2[1], h3c, ntile=NQ2 // 512)

    h3_ext = kb.dram([EXT, DM], F32)
    kb.phase_blend(flags, [(h3_ext[0:HALO, :], None, 0, h3c[0:HALO, :], 1),
                           (h3_ext[HALO:HALO + TOK, :], h3c[0:TOK, :], 0, h3c[HALO:HALO + TOK, :], 1),
                           (h3_ext[HALO + TOK:EXT, :], h3c[TOK:NQ2, :], 0, None, 1)])
    QTg3 = [kb.dram([DM, A_DIL[g], TOK // A_DIL[g]], BF16) for g in range(3)]
    KTg3 = [kb.dram([DM, A_DIL[g], TOK // A_DIL[g] + 128], BF16) for g in range(3)]
    Vg3 = [kb.dram([EXT, 8, 256], BF16) for g in range(3)]
    aT3 = kb.dram([DM, TOK], BF16)
    kb.phase_m1_a(h3_ext, g1[3, :], wq3[0], wq3[1], a_qg[1], a_kg[1], cosA3, sinA3, validA3, QTg3, KTg3, Vg3, TOKn=TOK)
    kb.phase_attn_a(QTg3, KTg3, Vg3, aT3, None, seg_off=0)
    kb.phase_out_ffn(h3_ext, HALO, aT3, wo3[0], wo3[1], g2[3, :], f1_3[0], f1_3[1], f2_3[0], f2_3[1], h_out)
    with nc.Block() as block:
        kb.S.emit(block)
    kb.es.close()
    return nc


_PROG = {}


def kernel(x, norm1, norm2, a_wqkv, a_q_gain, a_k_gain, a_wo, b_win, b_conv, b_wout,
           c_wqkv, c_q_gain, c_k_gain, c_wo, mlp_w1, mlp_w2):
    f = lambda a: np.ascontiguousarray(np.asarray(a, dtype=np.float32))
    x = f(x)
    if "nc" not in _PROG:
        _PROG["nc"] = build_fused()
    nc = _PROG["nc"]
    common = {
        "g1": f(norm1), "g2": f(norm2),
        "a_wqkv0": f(a_wqkv[0]), "a_wqkv1": f(a_wqkv[1]), "a_wo0": f(a_wo[0]), "a_wo1": f(a_wo[1]),
        "a_qg0": f(a_q_gain[0]), "a_qg1": f(a_q_gain[1]), "a_kg0": f(a_k_gain[0]), "a_kg1": f(a_k_gain[1]),
        "b_win": f(b_win[0]), "b_wout": f(b_wout[0]), "b_conv": f(b_conv[0]),
        "c_wqkv": f(c_wqkv[0]), "c_wo": f(c_wo[0]), "c_qg": f(c_q_gain[0]), "c_kg": f(c_k_gain[0]),
        "ident": np.eye(128, dtype=np.float32).astype(NPBF), "onesbd": onesbd_matrix(),
        "rotA": rot_matrix("A"), "rotC": rot_matrix("C"), "maskA": band_mask(),
    }
    for i in range(4):
        common["w1_%d" % i] = f(mlp_w1[i])
        common["w2_%d" % i] = f(mlp_w2[i])
    pos0 = np.arange(-HALO, SEQ + HALO)
    cosA0, sinA0 = rope_tables_a(pos0)
    valid0 = ((pos0 >= 0) & (pos0 < SEQ)).astype(np.float32)
    common.update({"cosA0": cosA0, "sinA0": sinA0,
                   "validA0": np.ascontiguousarray(valid0.reshape(SEQX // 128, 128).T)})
    in_maps = []
    for core in range(NCORES):
        b, half = core // 2, core % 2
        m = dict(common)
        x_ext = np.zeros((SEQX, DM), np.float32)
        x_ext[HALO:HALO + SEQ] = x[b]
        pos3 = np.arange(half * TOK - HALO, half * TOK - HALO + EXT)
        cosA3, sinA3 = rope_tables_a(pos3)
        valid3 = ((pos3 >= 0) & (pos3 < SEQ)).astype(np.float32)
        order = (np.arange(SEQ) + (SEQ - NQ2) * half) % SEQ
        cosC, sinC = rope_tables_c(order)
        fl = np.zeros((128, 2), np.float32)
        fl[:, half] = 1.0
        m.update({"x_ext": x_ext, "cosA3": cosA3, "sinA3": sinA3,
                  "validA3": np.ascontiguousarray(valid3.reshape(EXT // 128, 128).T),
                  "cosC": cosC, "sinC": sinC, "flags": fl})
        in_maps.append(m)
    res = run_bass_kernel_spmd(nc, in_maps, core_ids=list(range(NCORES)))
    out = np.empty((NB, SEQ, DM), np.float32)
    for core in range(NCORES):
        b, half = core // 2, core % 2
        out[b, half * TOK:(half + 1) * TOK] = res.results[core]["h_out"]
    return out
```
